# Optimizing a Trainium2 kernel written in Bass

```python
import jax, jax.numpy as jnp
from jax import lax
import numpy as np

D_MODEL = 1024
BATCH = 2
SEQ = 16384
DEPTH = 4

CHUNK = 64
EPS = 1e-6
GMLP_BLOCK = 128
A_HEADS = 4
A_WIDTH = D_MODEL // 2
A_HEAD_DIM = A_WIDTH // A_HEADS
POOL_WINDOWS = (2, 4, 8, 16)
B_GROUPS = len(POOL_WINDOWS)
B_WIDTH = D_MODEL // 2
B_GROUP_DIM = B_WIDTH // B_GROUPS
AB_IN_WIDTH = 2 * A_WIDTH + B_WIDTH
AB_OUT_WIDTH = A_WIDTH + B_WIDTH
RWKV_HEAD_DIM = 64
RWKV_HEADS = D_MODEL // RWKV_HEAD_DIM
DECAY_RANK = max(32, round(1.8 * D_MODEL ** 0.5 / 32) * 32)
AAA_RANK = max(32, round(1.8 * D_MODEL ** 0.5 / 32) * 32)
MV_RANK = max(32, round(1.3 * D_MODEL ** 0.5 / 32) * 32)
GATE_RANK = max(32, round(0.6 * D_MODEL ** 0.8 / 32) * 32)
GN_EPS = 64e-5
FFN_HIDDEN = ((8 * D_MODEL // 3 + 255) // 256) * 256
N_EVEN = (DEPTH + 1) // 2
N_ODD = DEPTH // 2

kernel_name = "hybrid_gmlp_pool_rwkv7_stream_encoder"


def rms_norm(x, g):
    xf = x.astype(jnp.float32)
    y = xf * lax.rsqrt(jnp.mean(xf * xf, axis=-1, keepdims=True) + EPS)
    return (y * g.astype(jnp.float32)).astype(x.dtype)


def gmlp_spatial_gate(u, v, sgu_gain, w_s, b_s):
    bsz, seq, _ = u.shape
    v = rms_norm(v, sgu_gain)
    v = v.reshape(bsz, seq // GMLP_BLOCK, GMLP_BLOCK, A_HEADS, A_HEAD_DIM)
    chunk_id = jnp.arange(GMLP_BLOCK) // CHUNK
    mask = chunk_id[:, None] >= chunk_id[None, :]
    w = jnp.where(mask[None], w_s, 0)
    mixed = jnp.einsum('hij,bnjhd->bnihd', w, v) + b_s.T[None, None, :, :, None]
    return u * mixed.reshape(bsz, seq, A_WIDTH)


def multiscale_pool(z, w_pool, pool_scale):
    bsz, seq, _ = z.shape
    zf = z.astype(jnp.float32).reshape(bsz, seq, B_GROUPS, B_GROUP_DIM)
    csum = jnp.cumsum(zf, axis=1)
    t = jnp.arange(seq)
    means = []
    for gi, win in enumerate(POOL_WINDOWS):
        cg = csum[:, :, gi]
        prev = jnp.pad(cg, ((0, 0), (win, 0), (0, 0)))[:, :seq]
        count = jnp.minimum(t + 1, win).astype(jnp.float32)[None, :, None]
        means.append((cg - prev) / count)
    pooled = (jnp.stack(means, axis=2) - zf).astype(z.dtype)
    y = jnp.einsum('bsgc,gcd->bsgd', pooled, w_pool).reshape(bsz, seq, B_WIDTH)
    return y * pool_scale


def hybrid_ab_mixer(hn, w_in, w_out, sgu_gain, sgu_w_s, sgu_bias, pool_w, pool_scale):
    z = hn @ w_in
    za = jax.nn.gelu(z[..., :2 * A_WIDTH], approximate=False)
    y_a = gmlp_spatial_gate(za[..., :A_WIDTH], za[..., A_WIDTH:], sgu_gain, sgu_w_s, sgu_bias)
    y_b = multiscale_pool(z[..., 2 * A_WIDTH:], pool_w, pool_scale)
    return jnp.concatenate([y_a, y_b], axis=-1) @ w_out


def wkv7_scan(r, w, k, v, a, b):
    bsz, _, nh, n = r.shape
    xs = tuple(jnp.moveaxis(t, 1, 0) for t in (r, w, k, v, a, b))

    def step(state, inp):
        r_t, w_t, k_t, v_t, a_t, b_t = inp
        sa = jnp.einsum('bhvk,bhk->bhv', state, a_t)
        state = (state * w_t[:, :, None, :] + sa[..., None] * b_t[:, :, None, :]
                 + v_t[..., None] * k_t[:, :, None, :])
        return state, jnp.einsum('bhvk,bhk->bhv', state, r_t)

    s0 = jnp.zeros((bsz, nh, n, n), jnp.float32)
    _, ys = lax.scan(step, s0, xs)
    return jnp.moveaxis(ys, 0, 1)


def rwkv7_time_mix(h, mu, w_r, w_k, w_v, w_o, w0, w1, w2, a0, a1, a2, g1, g2,
                   k_k, k_a, r_k, ln_w, ln_b, v_first, vres):
    bsz, seq, d = h.shape
    f32 = jnp.float32
    xx = jnp.pad(h, ((0, 0), (1, 0), (0, 0)))[:, :-1] - h
    xr, xw, xk, xv, xa, xg = [h + xx * mu[i] for i in range(6)]
    r = xr @ w_r
    wlog = -jax.nn.softplus(-(w0 + jnp.tanh(xw @ w1) @ w2).astype(f32)) - 0.5
    k = xk @ w_k
    v = xv @ w_v
    if vres is None:
        v_first = v
    else:
        v0, v1, v2 = vres
        v = v + (v_first - v) * jax.nn.sigmoid(v0 + (xv @ v1) @ v2)
    a = jax.nn.sigmoid(a0 + (xa @ a1) @ a2)
    g = jax.nn.sigmoid(xg @ g1) @ g2

    def heads(t):
        return t.reshape(bsz, seq, RWKV_HEADS, RWKV_HEAD_DIM).astype(f32)

    kk = heads(k * k_k)
    kk = kk / jnp.maximum(jnp.sqrt(jnp.sum(kk * kk, axis=-1, keepdims=True)), 1e-12)
    k = k * (1 + (a - 1) * k_a)
    rh, kh, vh, ah = heads(r), heads(k), heads(v), heads(a)
    decay = heads(jnp.exp(-jnp.exp(wlog)))
    y = wkv7_scan(rh, decay, kh, vh, -kk, kk * ah)
    mean = jnp.mean(y, axis=-1, keepdims=True)
    var = jnp.mean(jnp.square(y - mean), axis=-1, keepdims=True)
    y = ((y - mean) * lax.rsqrt(var + GN_EPS)).reshape(bsz, seq, d)
    y = y * ln_w.astype(f32) + ln_b.astype(f32)
    bonus = jnp.sum(rh * kh * r_k.astype(f32), axis=-1, keepdims=True) * vh
    y = (y + bonus.reshape(bsz, seq, d)).astype(h.dtype)
    return (y * g) @ w_o, v_first


def swiglu(hn, w_gate, w_up, w_down):
    return (jax.nn.silu(hn @ w_gate) * (hn @ w_up)) @ w_down


def setup_inputs(seed: int = 0) -> dict:
    key = jax.random.key(seed)
    ks = iter(jax.random.split(key, 48))
    D = D_MODEL

    def nrm(shape, scale):
        return scale * jax.random.normal(next(ks), shape, jnp.float32)

    def uni(shape, lo, hi):
        return jax.random.uniform(next(ks), shape, jnp.float32, lo, hi)

    nv = max(N_ODD - 1, 0)
    return {
        "x": nrm((BATCH, SEQ, D), 1.0),
        "mix_norm": 1.0 + nrm((DEPTH, D), 0.02),
        "ffn_norm": 1.0 + nrm((DEPTH, D), 0.02),
        "final_norm": 1.0 + nrm((D,), 0.02),
        "ab_w_in": nrm((N_EVEN, D, AB_IN_WIDTH), D ** -0.5),
        "ab_w_out": nrm((N_EVEN, AB_OUT_WIDTH, D), AB_OUT_WIDTH ** -0.5),
        "sgu_gain": 1.0 + nrm((N_EVEN, A_WIDTH), 0.02),
        "sgu_w_s": nrm((N_EVEN, A_HEADS, GMLP_BLOCK, GMLP_BLOCK), GMLP_BLOCK ** -0.5),
        "sgu_bias": 1.0 + nrm((N_EVEN, A_HEADS, GMLP_BLOCK), 0.1),
        "pool_w": nrm((N_EVEN, B_GROUPS, B_GROUP_DIM, B_GROUP_DIM), B_GROUP_DIM ** -0.5),
        "pool_scale": 0.5 + nrm((N_EVEN, B_WIDTH), 0.1),
        "rwkv_mu": uni((N_ODD, 6, D), 0.0, 1.0),
        "rwkv_w_r": nrm((N_ODD, D, D), D ** -0.5),
        "rwkv_w_k": nrm((N_ODD, D, D), D ** -0.5),
        "rwkv_w_v": nrm((N_ODD, D, D), D ** -0.5),
        "rwkv_w_o": nrm((N_ODD, D, D), D ** -0.5),
        "rwkv_w0": uni((N_ODD, D), -6.0, 0.0),
        "rwkv_w1": nrm((N_ODD, D, DECAY_RANK), D ** -0.5),
        "rwkv_w2": nrm((N_ODD, DECAY_RANK, D), 0.1 * DECAY_RANK ** -0.5),
        "rwkv_a0": nrm((N_ODD, D), 0.1),
        "rwkv_a1": nrm((N_ODD, D, AAA_RANK), D ** -0.5),
        "rwkv_a2": nrm((N_ODD, AAA_RANK, D), 0.1 * AAA_RANK ** -0.5),
        "rwkv_g1": nrm((N_ODD, D, GATE_RANK), D ** -0.5),
        "rwkv_g2": nrm((N_ODD, GATE_RANK, D), GATE_RANK ** -0.5),
        "rwkv_k_k": 0.85 + nrm((N_ODD, D), 0.05),
        "rwkv_k_a": 1.0 + nrm((N_ODD, D), 0.05),
        "rwkv_r_k": nrm((N_ODD, RWKV_HEADS, RWKV_HEAD_DIM), 0.1),
        "rwkv_ln_w": 1.0 + nrm((N_ODD, D), 0.05),
        "rwkv_ln_b": nrm((N_ODD, D), 0.02),
        "rwkv_v0": nrm((nv, D), 0.5),
        "rwkv_v1": nrm((nv, D, MV_RANK), D ** -0.5),
        "rwkv_v2": nrm((nv, MV_RANK, D), 0.1 * MV_RANK ** -0.5),
        "ffn_w_gate": nrm((DEPTH, D, FFN_HIDDEN), D ** -0.5),
        "ffn_w_up": nrm((DEPTH, D, FFN_HIDDEN), D ** -0.5),
        "ffn_w_down": nrm((DEPTH, FFN_HIDDEN, D), FFN_HIDDEN ** -0.5),
    }


def reference(x, mix_norm, ffn_norm, final_norm, ab_w_in, ab_w_out, sgu_gain, sgu_w_s,
              sgu_bias, pool_w, pool_scale, rwkv_mu, rwkv_w_r, rwkv_w_k, rwkv_w_v,
              rwkv_w_o, rwkv_w0, rwkv_w1, rwkv_w2, rwkv_a0, rwkv_a1, rwkv_a2, rwkv_g1,
              rwkv_g2, rwkv_k_k, rwkv_k_a, rwkv_r_k, rwkv_ln_w, rwkv_ln_b, rwkv_v0,
              rwkv_v1, rwkv_v2, ffn_w_gate, ffn_w_up, ffn_w_down):
    h = x
    v_first = None
    for layer in range(DEPTH):
        hn = rms_norm(h, mix_norm[layer])
        i = layer // 2
        if layer % 2 == 0:
            h = h + hybrid_ab_mixer(hn, ab_w_in[i], ab_w_out[i], sgu_gain[i], sgu_w_s[i],
                                    sgu_bias[i], pool_w[i], pool_scale[i])
        else:
            vres = None if i == 0 else (rwkv_v0[i - 1], rwkv_v1[i - 1], rwkv_v2[i - 1])
            y, v_first = rwkv7_time_mix(
                hn, rwkv_mu[i], rwkv_w_r[i], rwkv_w_k[i], rwkv_w_v[i], rwkv_w_o[i],
                rwkv_w0[i], rwkv_w1[i], rwkv_w2[i], rwkv_a0[i], rwkv_a1[i], rwkv_a2[i],
                rwkv_g1[i], rwkv_g2[i], rwkv_k_k[i], rwkv_k_a[i], rwkv_r_k[i],
                rwkv_ln_w[i], rwkv_ln_b[i], v_first, vres)
            h = h + y
        h = h + swiglu(rms_norm(h, ffn_norm[layer]), ffn_w_gate[layer], ffn_w_up[layer],
                       ffn_w_down[layer])
    return rms_norm(h, final_norm)
```

```python
import numpy as np
from contextlib import ExitStack
import concourse.bass as bass
import concourse.mybir as mybir
from concourse.bass_utils import run_bass_kernel_spmd

F32 = mybir.dt.float32
BF16 = mybir.dt.bfloat16
AF = mybir.ActivationFunctionType
ALU = mybir.AluOpType
AX = mybir.AxisListType

D = 1024
KC = 8
FH = 2816
HC = 22
NCORES = 8
EPS = 1e-6


class Buf:
    __slots__ = ("name", "w", "r", "sem_key", "base")

    def __init__(self, name, base=None):
        self.name = name
        self.base = base if base is not None else name
        self.w = None
        self.r = {}
        self.sem_key = None


class Op:
    __slots__ = ("eng", "fn", "deps", "dma", "sig", "need", "idx", "inc")

    def __init__(self, eng, fn, dma):
        self.inc = 16
        self.eng = eng
        self.fn = fn
        self.deps = []
        self.dma = dma
        self.sig = None
        self.need = False


ENGS = ("pe", "act", "dve", "pool", "sp")
CC_INC = 1


class _Rec:
    def __init__(self):
        self.call = None

    def __getattr__(self, name):
        def f(*a, **k):
            self.call = (name, a, k)
            return None
        return f


class Prog:
    SB_LO = 16512
    SB_HI = 229376

    def __init__(self, nc, stack):
        self.nc = nc
        self.stack = stack
        self.ops = {e: [] for e in ENGS}
        self.sems = {}
        self.dma_cnt = {}
        self.nbuf = 0
        self.sb_ptr = self.SB_LO
        self.sb_mark = self.SB_LO
        self.pending_dma = []
        self.nalloc = 0

    def alloc(self, name, shape, dt):
        esz = 4 if dt == F32 else 2
        n = 1
        for d in shape[1:]:
            n *= d
        nbytes = (n * esz + 63) // 64 * 64
        off = self.sb_ptr
        assert off + nbytes <= self.SB_HI, "SBUF overflow: %s needs %d at %d" % (name, nbytes, off)
        self.sb_ptr += nbytes
        self.nalloc += 1
        t = self.nc.alloc_sbuf_tensor_at("%s_%d" % (name, self.nalloc), list(shape), dt, offset=off)
        return t, self.buf(name)

    def persist(self):
        self.sb_mark = self.sb_ptr

    def phase_reset(self):
        lasts = []
        for e in ENGS:
            for o in reversed(self.ops[e]):
                if not o.dma and o.fn is not None:
                    lasts.append(o)
                    break
        deps = lasts + self.pending_dma
        self.pending_dma = []
        for e in ENGS:
            o = Op(e, None, False)
            for d in deps:
                if d.eng == e and not d.dma:
                    continue
                o.deps.append(d)
                d.need = True
            self.ops[e].append(o)
        self.sb_ptr = self.sb_mark

    def sem(self, key):
        if key not in self.sems:
            self.sems[key] = self.stack.enter_context(self.nc.semaphore("s_%s" % (str(key).replace(" ", "_"))))
        return self.sems[key]

    def buf(self, name):
        self.nbuf += 1
        return Buf("%s_%d" % (name, self.nbuf), name)

    def sb(self, name, shape, dt):
        return self.alloc(name, shape, dt)

    def ps(self, name, shape=(128, 512), dt=F32):
        t = self.stack.enter_context(self.nc.psum_tensor(name, list(shape), dt))
        return t, self.buf(name)

    def _add(self, op, reads, writes):
        deps = []
        for b in reads:
            if b.w is not None:
                deps.append(b.w)
        for b in writes:
            if b.w is not None:
                deps.append(b.w)
            deps.extend(b.r.values())
        seen = set()
        for d in deps:
            if d is op or id(d) in seen:
                continue
            seen.add(id(d))
            if (not d.dma) and d.eng == op.eng and op.eng == "pe" and not op.dma:
                continue
            op.deps.append(d)
            d.need = True
        for b in reads:
            b.r[op.sig[0] if op.dma else op.eng] = op
        for b in writes:
            b.w = op
            b.r = {}
        self.ops[op.eng].append(op)
        return op

    def op(self, eng, fn, reads=(), writes=()):
        r = _Rec()
        fn(r)
        assert r.call is not None
        name, a, k = r.call
        return self._add(Op(eng, lambda e, name=name, a=a, k=k: getattr(e, name)(*a, **k), False), reads, writes)

    def dma(self, eng, out_ap, in_ap, reads=(), writes=(), semkey=None, slow=False):
        kb = semkey if semkey is not None else (writes[0] if writes else reads[0])
        key = ("d", kb.base, "w" if writes and kb is writes[0] else "r")
        self.sem(key)
        n = self.dma_cnt.get(key, 0) + 1
        self.dma_cnt[key] = n
        if slow:
            o = Op(eng, lambda e, out_ap=out_ap, in_ap=in_ap: e.dma_start(out=out_ap, in_=in_ap, allow_slow_non_contiguous=True), True)
        else:
            o = Op(eng, lambda e, out_ap=out_ap, in_ap=in_ap: e.dma_start(out=out_ap, in_=in_ap), True)
        o.sig = (key, 16 * n)
        o.need = True
        self.pending_dma.append(o)
        return self._add(o, reads, writes)

    def coll(self, kind, in_ap, out_ap, groups, reads=(), writes=()):
        key = ("cc",)
        self.sem(key)
        n = self.dma_cnt.get(key, 0) + 1
        self.dma_cnt[key] = n
        o = Op("pool", lambda e: e.collective_compute(kind, ALU.bypass, replica_groups=groups, ins=[in_ap.opt()], outs=[out_ap.opt()]), True)
        o.inc = CC_INC
        o.sig = (key, CC_INC * n)
        o.need = True
        self.pending_dma.append(o)
        return self._add(o, reads, writes)

    def emit(self, block, final_waits=()):
        nc = self.nc
        EPOCH = 3000
        for e in ENGS:
            c = 0
            ep = 0
            for o in self.ops[e]:
                if o.dma:
                    continue
                if o.need:
                    c += 1
                    if c > EPOCH:
                        c = 1
                        ep += 1
                    o.sig = (("eng", e, ep), c)
                    self.sem(("eng", e, ep))
        self.maxsig = {e: 0 for e in ENGS}

        def run(engname, eng):
            seen = {}
            for o in self.ops[engname]:
                mx = {}
                for d in o.deps:
                    k, v = d.sig
                    if v > mx.get(k, 0):
                        mx[k] = v
                for k, v in mx.items():
                    if seen.get(k, 0) >= v:
                        continue
                    seen[k] = v
                    eng.wait_ge(self.sems[k], v)
                if o.fn is None:
                    continue
                ins = o.fn(eng)
                if o.need:
                    k, v = o.sig
                    if o.dma:
                        ins.then_inc(self.sems[k], o.inc)
                    else:
                        ins.then_inc(self.sems[k], 1)
            if engname == "sp":
                mx = {}
                for o in final_waits:
                    k, v = o.sig
                    if v > mx.get(k, 0):
                        mx[k] = v
                for k, v in mx.items():
                    if seen.get(k, 0) < v:
                        eng.wait_ge(self.sems[k], v)

        @block.tensor
        def _(eng):
            run("pe", eng)

        @block.scalar
        def _(eng):
            run("act", eng)

        @block.vector
        def _(eng):
            run("dve", eng)

        @block.gpsimd
        def _(eng):
            run("pool", eng)

        @block.sync
        def _(eng):
            run("sp", eng)


class Ctx:
    def __init__(self, P, T):
        self.P = P
        self.T = T
        nc = P.nc
        self.banks = [P.ps("bank%d" % i) for i in range(8)]
        self.bank_i = 0
        self.ones, self.ones_b = P.sb("ones_mean", (128, 128), BF16)
        P.op("pool", lambda e: e.memset(self.ones[:], 1.0 / D), writes=[self.ones_b])
        self.eps_t, self.eps_b = P.sb("eps", (128, 2), F32)
        P.op("pool", lambda e: e.memset(self.eps_t[:], EPS), writes=[self.eps_b])
        self.dq = 0

    def bank(self):
        for _ in range(8):
            t, b = self.banks[self.bank_i % 8]
            self.bank_i += 1
            if b.w is None or b.r:
                return t, b
        raise RuntimeError("all 8 PSUM banks hold unconsumed results")

    def dma_eng(self):
        self.dq += 1
        return "sp"


def load_vec_cols(P, dram_vec, name, n_chunks):
    t, b = P.sb(name, (128, n_chunks), F32)
    src = dram_vec.rearrange("(k p) -> p k", p=128)
    P.dma("sp", t[:], src, writes=[b], slow=True)
    return t, b


def rmsnorm_tile(P, C, h, hb, gain, gainb, hn, hnb, sq, sqb, rstd, rstdb, T):
    for kc in range(KC):
        P.op("pool" if kc % 2 else "dve",
             lambda e, kc=kc: e.tensor_tensor(out=sq[:, kc, :T], in0=h[:, kc, :T], in1=h[:, kc, :T], op=ALU.mult),
             reads=[hb], writes=[sqb[kc]])
    pt, pb = C.bank()
    for kc in range(KC):
        P.op("pe", lambda e, kc=kc: e.matmul(pt[:, :T], C.ones[:], sq[:, kc, :T], start=(kc == 0), stop=(kc == KC - 1)),
             reads=[sqb[kc], C.ones_b], writes=[pb])
    P.op("act", lambda e: e.activation(out=rstd[:, :T], in_=pt[:, :T], func=AF.Sqrt, bias=C.eps_t[:, 0:1]),
         reads=[pb, C.eps_b], writes=[rstdb])
    P.op("dve", lambda e: e.reciprocal(out=rstd[:, :T], in_=rstd[:, :T]), reads=[rstdb], writes=[rstdb])
    for kc in range(KC):
        P.op("dve",
             lambda e, kc=kc: e.scalar_tensor_tensor(out=hn[:, kc, :T], in0=h[:, kc, :T], scalar=gain[:, kc:kc + 1],
                                                     in1=rstd[:, :T], op0=ALU.mult, op1=ALU.mult),
             reads=[hb, rstdb, gainb], writes=[hnb[kc]])


def phase_ffn(P, C, hT_in, hT_out, w_gate, w_up, w_down, gain_vec, NT, final_gain=None, outT=None, halo_out=None):
    nc = P.nc
    T = C.T
    sbl = P.alloc

    wg, wgb = sbl("wg", (128, KC, FH), BF16)
    wu, wub = sbl("wu", (128, KC, FH), BF16)
    wd, wdb = sbl("wd", (128, HC, D), BF16)
    P.dma("pool", wg[:], w_gate.rearrange("(k p) n -> p k n", p=128), writes=[wgb])
    P.dma("pool", wu[:], w_up.rearrange("(k p) n -> p k n", p=128), writes=[wub])
    P.dma("pool", wd[:], w_down.rearrange("(c p) n -> p c n", p=128), writes=[wdb])
    gain, gainb = sbl("ffn_gain", (128, KC), F32)
    P.dma("sp", gain[:], gain_vec.rearrange("(k p) -> p k", p=128), writes=[gainb], slow=True)
    if final_gain is not None:
        fg, fgb = sbl("fin_gain", (128, KC), F32)
        P.dma("sp", fg[:], final_gain.rearrange("(k p) -> p k", p=128), writes=[fgb], slow=True)

    hs = [sbl("ffn_h%d" % i, (128, KC, T), F32) for i in range(2)]
    hn, _ = sbl("ffn_hn", (128, KC, T), BF16)
    hnb = [P.buf("hn%d" % k) for k in range(KC)]
    rstd, rstdb = sbl("ffn_rstd", (128, T), F32)
    act, _ = sbl("ffn_act", (128, HC, T), BF16)
    actb = [P.buf("act%d" % k) for k in range(HC)]
    sq, sqb = act, actb
    sg = [sbl("ffn_sg%d" % i, (128, T), F32) for i in range(2)]

    hin = hT_in.rearrange("(k p) t -> p k t", p=128)
    hout = hT_out.rearrange("(k p) t -> p k t", p=128) if hT_out is not None else None
    stores = []
    for it in range(NT):
        h, hb = hs[it % 2]
        t0 = it * T
        P.dma("sp", h[:], hin[:, :, t0:t0 + T], writes=[hb])
        rmsnorm_tile(P, C, h, hb, gain, gainb, hn, hnb, sq, sqb, rstd, rstdb, T)
        for hc in range(HC):
            pg, pgb = C.bank()
            pu, pub = C.bank()
            for kc in range(KC):
                P.op("pe", lambda e, kc=kc, hc=hc, pg=pg: e.matmul(pg[:, :T], wg[:, kc, hc * 128:(hc + 1) * 128], hn[:, kc, :T],
                                                                    start=(kc == 0), stop=(kc == KC - 1)),
                     reads=[wgb, hnb[kc]], writes=[pgb])
            for kc in range(KC):
                P.op("pe", lambda e, kc=kc, hc=hc, pu=pu: e.matmul(pu[:, :T], wu[:, kc, hc * 128:(hc + 1) * 128], hn[:, kc, :T],
                                                                    start=(kc == 0), stop=(kc == KC - 1)),
                     reads=[wub, hnb[kc]], writes=[pub])
            s, sb_ = sg[hc % 2]
            P.op("act", lambda e, s=s, pg=pg: e.activation(out=s[:, :T], in_=pg[:, :T], func=AF.Silu),
                 reads=[pgb], writes=[sb_])
            P.op("dve", lambda e, s=s, pu=pu, hc=hc: e.tensor_tensor(out=act[:, hc, :T], in0=pu[:, :T], in1=s[:, :T], op=ALU.mult),
                 reads=[pub, sb_], writes=[actb[hc]])
        for dc in range(KC):
            po, pob = C.bank()
            for hc in range(HC):
                P.op("pe", lambda e, hc=hc, dc=dc, po=po: e.matmul(po[:, :T], wd[:, hc, dc * 128:(dc + 1) * 128], act[:, hc, :T],
                                                                    start=(hc == 0), stop=(hc == HC - 1)),
                     reads=[wdb, actb[hc]], writes=[pob])
            P.op("dve", lambda e, dc=dc, po=po, h=h: e.tensor_tensor(out=h[:, dc, :T], in0=po[:, :T], in1=h[:, dc, :T], op=ALU.add),
                 reads=[pob, hb], writes=[hb])
        if final_gain is None:
            stores.append(P.dma("sp", hout[:, :, t0:t0 + T], h[:], reads=[hb], writes=[]))
            if halo_out is not None and it == NT - 1:
                stores.append(P.dma("sp", halo_out.rearrange("(k p) c -> p k c", p=128), h[:, :, T - HALO:T], reads=[hb], writes=[]))
        else:
            hnf, _ = hn, None
            for kc in range(KC):
                P.op("pool" if kc % 2 else "dve",
                     lambda e, kc=kc, h=h: e.tensor_tensor(out=sq[:, kc, :T], in0=h[:, kc, :T], in1=h[:, kc, :T], op=ALU.mult),
                     reads=[hb], writes=[sqb[kc]])
            pt, pb = C.bank()
            for kc in range(KC):
                P.op("pe", lambda e, kc=kc, pt=pt: e.matmul(pt[:, :T], C.ones[:], sq[:, kc, :T], start=(kc == 0), stop=(kc == KC - 1)),
                     reads=[sqb[kc], C.ones_b], writes=[pb])
            P.op("act", lambda e, pt=pt: e.activation(out=rstd[:, :T], in_=pt[:, :T], func=AF.Sqrt, bias=C.eps_t[:, 0:1]),
                 reads=[pb, C.eps_b], writes=[rstdb])
            P.op("dve", lambda e: e.reciprocal(out=rstd[:, :T], in_=rstd[:, :T]), reads=[rstdb], writes=[rstdb])
            for kc in range(KC):
                P.op("dve",
                     lambda e, kc=kc, h=h: e.scalar_tensor_tensor(out=h[:, kc, :T], in0=h[:, kc, :T], scalar=fg[:, kc:kc + 1],
                                                                  in1=rstd[:, :T], op0=ALU.mult, op1=ALU.mult),
                     reads=[hb, rstdb, fgb], writes=[hb])
            oo = outT.rearrange("(k p) t -> p k t", p=128)
            stores.append(P.dma("sp", oo[:, :, t0:t0 + T], h[:], reads=[hb], writes=[]))
    return stores


HALO = 16
AW = 512


def phase_ab(P, C, hT_in, hT_out, w_in, w_out, mixgain_vec, sgu_gain_row, wsT_d, sgu_bias_row, pool_w_d, pool_scale_vec,
             invc_d, NT):
    nc = P.nc
    T = C.T
    TH = T + HALO
    A = P.alloc
    win, winb = A("ab_win", (128, KC, 3 * AW), BF16)
    wout, woutb = A("ab_wout", (128, KC, D), BF16)
    P.dma("pool", win[:], w_in.rearrange("(k p) n -> p k n", p=128), writes=[winb])
    P.dma("pool", wout[:], w_out.rearrange("(k p) n -> p k n", p=128), writes=[woutb])
    wsT, wsTb = A("ab_wsT", (128, 4, 128), BF16)
    P.dma("pool", wsT[:], wsT_d, writes=[wsTb])
    P.op("pool", lambda e: e.memset(wsT[64:128, :, 0:64], 0.0), reads=[wsTb], writes=[wsTb])
    pw, pwb = A("ab_pw", (128, 4, 128), BF16)
    P.dma("pool", pw[:], pool_w_d, writes=[pwb])
    gain, gainb = A("ab_gain", (128, KC), F32)
    P.dma("sp", gain[:], mixgain_vec.rearrange("(k p) -> p k", p=128), writes=[gainb], slow=True)
    psc, pscb = A("ab_psc", (128, 4), F32)
    P.dma("sp", psc[:], pool_scale_vec.rearrange("(k p) -> p k", p=128), writes=[pscb], slow=True)
    invc, invcb = A("ab_invc", (128, 4, HALO), F32)
    P.dma("sp", invc[:], invc_d, writes=[invcb])
    rows, rowsb = A("ab_rows", (1, 2 * AW), F32)
    P.dma("sp", rows[0:1, 0:AW], sgu_gain_row, writes=[rowsb])
    P.dma("sp", rows[0:1, AW:2 * AW], sgu_bias_row, writes=[rowsb])
    onesf, onesfb = A("ab_onesf", (1, 128), F32)
    P.op("pool", lambda e: e.memset(onesf[:], 1.0), writes=[onesfb])
    gbc, gbcb = A("ab_gbc", (128, AW), F32)
    pt, pb = C.bank()
    P.op("pe", lambda e: e.matmul(pt[:, :AW], onesf[0:1, :], rows[0:1, 0:AW], start=True, stop=True),
         reads=[onesfb, rowsb], writes=[pb])
    P.op("dve", lambda e: e.tensor_copy(out=gbc[:], in_=pt[:, :AW]), reads=[pb], writes=[gbcb])

    hs = [A("ab_h%d" % i, (128, KC, TH), F32) for i in range(2)]
    hn, _ = A("ab_hn", (128, KC, TH), BF16)
    hnb = [P.buf("abhn%d" % k) for k in range(KC)]
    sq, _ = A("ab_sq", (128, KC, TH), BF16)
    sqb = [P.buf("absq%d" % k) for k in range(KC)]
    rstd, rstdb = A("ab_rstd", (128, TH), F32)
    u, _ = A("ab_u", (128, 4, T), BF16)
    ub = [P.buf("abu%d" % k) for k in range(4)]
    vs = [A("ab_v%d" % i, (128, AW), F32) for i in range(4)]
    vq = [A("ab_vq%d" % i, (128, AW), F32) for i in range(4)]
    vn = [A("ab_vn%d" % i, (128, AW), BF16) for i in range(4)]
    ss = [A("ab_ss%d" % i, (128, 2), F32) for i in range(4)]
    zp, _ = A("ab_zp", (128, 4, TH), F32)
    zpb = [P.buf("abzp%d" % k) for k in range(4)]
    sa_all = [[A("ab_sa%d_%d" % (g_, i), (128, TH), F32) for i in range(2)] for g_ in range(4)]
    pooled, _ = A("ab_pooled", (128, 4, T), BF16)
    pooledb = [P.buf("abpl%d" % k) for k in range(4)]
    ptmps = [A("ab_ptmp%d" % g_, (128, HALO), F32) for g_ in range(4)]
    cat, _ = A("ab_cat", (128, KC, T), BF16)
    catb = [P.buf("abcat%d" % k) for k in range(KC)]

    hin = hT_in.rearrange("(k p) t -> p k t", p=128)
    hout = hT_out.rearrange("(k p) t -> p k t", p=128)
    stores = []
    for it in range(NT):
        h, hb = hs[it % 2]
        t0 = it * T
        P.dma("sp", h[:], hin[:, :, t0:t0 + TH], writes=[hb])
        for kc in range(KC):
            P.op("pool" if kc % 2 else "dve",
                 lambda e, kc=kc, h=h: e.tensor_tensor(out=sq[:, kc, :], in0=h[:, kc, :], in1=h[:, kc, :], op=ALU.mult),
                 reads=[hb], writes=[sqb[kc]])
        for (c0, c1) in ((0, HALO), (HALO, TH)):
            pt, pb = C.bank()
            n = c1 - c0
            for kc in range(KC):
                P.op("pe", lambda e, kc=kc, pt=pt, c0=c0, c1=c1, n=n: e.matmul(pt[:, :n], C.ones[:], sq[:, kc, c0:c1],
                                                                                 start=(kc == 0), stop=(kc == KC - 1)),
                     reads=[sqb[kc], C.ones_b], writes=[pb])
            P.op("act", lambda e, pt=pt, c0=c0, c1=c1, n=n: e.activation(out=rstd[:, c0:c1], in_=pt[:, :n], func=AF.Sqrt,
                                                                          bias=C.eps_t[:, 0:1]),
                 reads=[pb, C.eps_b], writes=[rstdb])
        P.op("dve", lambda e: e.reciprocal(out=rstd[:], in_=rstd[:]), reads=[rstdb], writes=[rstdb])
        for kc in range(KC):
            P.op("dve", lambda e, kc=kc, h=h: e.scalar_tensor_tensor(out=hn[:, kc, :], in0=h[:, kc, :], scalar=gain[:, kc:kc + 1],
                                                                     in1=rstd[:], op0=ALU.mult, op1=ALU.mult),
                 reads=[hb, rstdb, gainb], writes=[hnb[kc]])
        for c in range(4):
            pt, pb = C.bank()
            for kc in range(KC):
                P.op("pe", lambda e, kc=kc, c=c, pt=pt: e.matmul(pt[:, :T], win[:, kc, c * 128:(c + 1) * 128], hn[:, kc, HALO:TH],
                                                                  start=(kc == 0), stop=(kc == KC - 1)),
                     reads=[winb, hnb[kc]], writes=[pb])
            P.op("act", lambda e, c=c, pt=pt: e.activation(out=u[:, c, :], in_=pt[:, :T], func=AF.Gelu),
                 reads=[pb], writes=[ub[c]])
        def pool_chain(g):
            sa = sa_all[g]
            ptmp, ptmpb = ptmps[g]
            pt, pb = C.bank()
            pt2, pb2 = C.bank()
            for kc in range(KC):
                P.op("pe", lambda e, kc=kc, g=g, pt=pt: e.matmul(pt[:, :T], win[:, kc, 2 * AW + g * 128:2 * AW + (g + 1) * 128],
                                                                  hn[:, kc, HALO:TH], start=(kc == 0), stop=(kc == KC - 1)),
                     reads=[winb, hnb[kc]], writes=[pb])
            for kc in range(KC):
                P.op("pe", lambda e, kc=kc, g=g, pt2=pt2: e.matmul(pt2[:, :HALO], win[:, kc, 2 * AW + g * 128:2 * AW + (g + 1) * 128],
                                                                    hn[:, kc, 0:HALO], start=(kc == 0), stop=(kc == KC - 1)),
                     reads=[winb, hnb[kc]], writes=[pb2])
            P.op("act", lambda e, g=g, pt=pt: e.copy(out=zp[:, g, HALO:TH], in_=pt[:, :T]), reads=[pb], writes=[zpb[g]])
            P.op("dve", lambda e, g=g, pt2=pt2: e.tensor_copy(out=zp[:, g, 0:HALO], in_=pt2[:, :HALO]), reads=[pb2, zpb[g]], writes=[zpb[g]])
            src, srcb = zp[:, g, :], zpb[g]
            lo = 0
            for lv in range(g + 1):
                step = 1 << lv
                dst, dstb = sa[lv % 2]
                lo2 = lo + step
                P.op("pool", lambda e, src=src, dst=dst, lo2=lo2, step=step: e.tensor_tensor(
                    out=dst[:, lo2:TH], in0=src[:, lo2:TH], in1=src[:, lo2 - step:TH - step], op=ALU.add),
                    reads=[srcb], writes=[dstb])
                src, srcb, lo = dst, dstb, lo2
                yield
            wsz = float(1 << (g + 1))
            P.op("dve", lambda e, src=src, g=g, wsz=wsz: e.scalar_tensor_tensor(
                out=pooled[:, g, :], in0=src[:, HALO:TH], scalar=1.0 / wsz, in1=zp[:, g, HALO:TH], op0=ALU.mult, op1=ALU.subtract),
                reads=[srcb, zpb[g]], writes=[pooledb[g]])
            if it == 0:
                P.op("dve", lambda e, src=src, g=g: e.tensor_tensor(out=ptmp[:], in0=src[:, HALO:2 * HALO], in1=invc[:, g, :], op=ALU.mult),
                     reads=[srcb, invcb], writes=[ptmpb])
                P.op("dve", lambda e, g=g: e.tensor_tensor(out=pooled[:, g, 0:HALO], in0=ptmp[:], in1=zp[:, g, HALO:2 * HALO], op=ALU.subtract),
                     reads=[ptmpb, zpb[g], pooledb[g]], writes=[pooledb[g]])
            yield
            pt, pb = C.bank()
            P.op("pe", lambda e, g=g, pt=pt: e.matmul(pt[:, :T], pw[:, g, :], pooled[:, g, :], start=True, stop=True),
                 reads=[pwb, pooledb[g]], writes=[pb])
            yield
            P.op("dve", lambda e, g=g, pt=pt: e.tensor_scalar(out=cat[:, 4 + g, :], in0=pt[:, :T], scalar1=psc[:, g:g + 1], scalar2=None,
                                                              op0=ALU.mult),
                 reads=[pb, pscb], writes=[catb[4 + g]])
        def gmlp_chain(n):
            v, vb = vs[n % 4]
            q_, qb = vq[n % 4]
            vnn, vnb = vn[n % 4]
            s_, sb_ = ss[n % 4]
            c0 = HALO + n * 128
            pt, pb = C.bank()
            for kc in range(KC):
                P.op("pe", lambda e, kc=kc, pt=pt, c0=c0: e.matmul(pt[:, :AW], hn[:, kc, c0:c0 + 128], win[:, kc, AW:2 * AW],
                                                                    start=(kc == 0), stop=(kc == KC - 1)),
                     reads=[winb, hnb[kc]], writes=[pb])
            yield
            P.op("act", lambda e, v=v, pt=pt: e.activation(out=v[:], in_=pt[:, :AW], func=AF.Gelu), reads=[pb], writes=[vb])
            yield
            P.op("pool", lambda e, v=v, q_=q_: e.tensor_tensor(out=q_[:], in0=v[:], in1=v[:], op=ALU.mult), reads=[vb], writes=[qb])
            yield
            P.op("dve", lambda e, q_=q_, s_=s_: e.reduce_sum(out=s_[:, 0:1], in_=q_[:], axis=AX.X), reads=[qb], writes=[sb_])
            P.op("act", lambda e, s_=s_: e.activation(out=s_[:, 1:2], in_=s_[:, 0:1], func=AF.Sqrt, bias=C.eps_t[:, 0:1], scale=1.0 / AW),
                 reads=[sb_, C.eps_b], writes=[sb_])
            P.op("dve", lambda e, s_=s_: e.reciprocal(out=s_[:, 1:2], in_=s_[:, 1:2]), reads=[sb_], writes=[sb_])
            yield
            P.op("dve", lambda e, v=v, vnn=vnn, s_=s_: e.scalar_tensor_tensor(out=vnn[:], in0=v[:], scalar=s_[:, 1:2], in1=gbc[:],
                                                                              op0=ALU.mult, op1=ALU.mult),
                 reads=[vb, sb_, gbcb], writes=[vnb])
            yield
            pm, pmb = C.bank()
            for hd in range(4):
                P.op("pe", lambda e, hd=hd, pm=pm, vnn=vnn: e.matmul(pm[:, hd * 128:(hd + 1) * 128], vnn[:, hd * 128:(hd + 1) * 128],
                                                                      wsT[:, hd, :], start=True, stop=False),
                     reads=[vnb, wsTb], writes=[pmb])
                P.op("pe", lambda e, hd=hd, pm=pm: e.matmul(pm[:, hd * 128:(hd + 1) * 128], onesf[0:1, :],
                                                             rows[0:1, AW + hd * 128:AW + (hd + 1) * 128], start=False, stop=True),
                     reads=[onesfb, rowsb], writes=[pmb])
            yield
            for hd in range(4):
                P.op("dve", lambda e, hd=hd, pm=pm, n=n: e.tensor_tensor(out=cat[:, hd, n * 128:(n + 1) * 128],
                                                                         in0=pm[:, hd * 128:(hd + 1) * 128],
                                                                         in1=u[:, hd, n * 128:(n + 1) * 128], op=ALU.mult),
                     reads=[pmb, ub[hd]], writes=[catb[hd]])
        gens = [pool_chain(g_) for g_ in range(4)] + [gmlp_chain(n_) for n_ in range(T // 128)]
        while gens:
            for gn_ in list(gens):
                try:
                    next(gn_)
                except StopIteration:
                    gens.remove(gn_)
        for dc in range(KC):
            po, pob = C.bank()
            for c in range(KC):
                P.op("pe", lambda e, c=c, dc=dc, po=po: e.matmul(po[:, :T], wout[:, c, dc * 128:(dc + 1) * 128], cat[:, c, :],
                                                                  start=(c == 0), stop=(c == KC - 1)),
                     reads=[woutb, catb[c]], writes=[pob])
            P.op("dve", lambda e, dc=dc, po=po, h=h: e.tensor_tensor(out=h[:, dc, HALO:TH], in0=po[:, :T], in1=h[:, dc, HALO:TH], op=ALU.add),
                 reads=[pob, hb], writes=[hb])
        stores.append(P.dma("sp", hout[:, :, t0:t0 + T], h[:, :, HALO:TH], reads=[hb], writes=[]))
    return stores


KAPPA = float(np.exp(-0.5))
RW_T = 128
NVEC = 15
V_GAIN, V_MU, V_W0, V_A0, V_KK, V_KA, V_RK, V_V0, V_LNW, V_LNB = 0, 1, 7, 8, 9, 10, 11, 12, 13, 14
CST_W = 1792 + 512


def make_cst():
    c = np.zeros((128, CST_W), np.float32)
    j = np.arange(128)[:, None]
    t = np.arange(128)[None, :]
    strict = (j < t).astype(np.float32)
    incl = (j <= t).astype(np.float32)
    c[:, 0:128] = strict
    c[:, 128:256] = incl
    c[:, 256:384] = strict
    c[:, 384:512] = incl
    low = (t < j).astype(np.float32)
    for u in range(4):
        c[:, 512 + u * 128:512 + (u + 1) * 128] = low
    c[:, 1024:1152] = np.eye(128, dtype=np.float32)
    c[0:64, 1152:1216] = 1.0
    c[64:128, 1216:1280] = 1.0
    seg = np.ones(512, np.float32)
    seg[0::128] = 0.0
    c[:, 1280:1792] = seg[None, :]
    for u in range(4):
        c[:, 1792 + u * 128:1792 + (u + 1) * 128] = np.eye(128, dtype=np.float32)
    return c


def phase_rwkv_pre(P, C, hT_in, Wd, vecs_d, cst_d, NT, T, y0_d, z_d, bonus_d, g_d, v_d, vfirst_d, state_d, has_vres, stage=9, KPAR=4):
    nc = P.nc
    A = P.alloc
    CH = T // 128
    T1 = T + 1
    NU = 2 * CH
    assert NU <= 4

    def wload(name, src, shape):
        t, b = A(name, shape, BF16)
        P.dma("pool", t[:], src, writes=[b])
        return t, b

    wr, wrb = wload("rw_wr", Wd["w_r"].rearrange("(k p) n -> p k n", p=128), (128, KC, D))
    wk, wkb = wload("rw_wk", Wd["w_k"].rearrange("(k p) n -> p k n", p=128), (128, KC, D))
    wv, wvb = wload("rw_wv", Wd["w_v"].rearrange("(k p) n -> p k n", p=128), (128, KC, D))
    w1, w1b = wload("rw_w1", Wd["w1"].rearrange("(k p) n -> p k n", p=128), (128, KC, 64))
    a1, a1b = wload("rw_a1", Wd["a1"].rearrange("(k p) n -> p k n", p=128), (128, KC, 64))
    g1, g1b = wload("rw_g1", Wd["g1"].rearrange("(k p) n -> p k n", p=128), (128, KC, 160))
    w2, w2b = wload("rw_w2", Wd["w2"], (64, D))
    a2, a2b = wload("rw_a2", Wd["a2"], (64, D))
    g2a, g2ab = wload("rw_g2a", Wd["g2"][0:128, :], (128, D))
    g2b, g2bb = wload("rw_g2b", Wd["g2"][128:160, :], (32, D))
    if has_vres:
        v1, v1b = wload("rw_v1", Wd["v1"].rearrange("(k p) n -> p k n", p=128), (128, KC, 32))
        v2, v2b = wload("rw_v2", Wd["v2"], (32, D))
    vecs, vecsb = A("rw_vecs", (128, NVEC * 8), F32)
    P.dma("sp", vecs[:], vecs_d, writes=[vecsb])
    cst, cstb = A("rw_cst", (128, CST_W), F32)
    P.dma("sp", cst[:], cst_d, writes=[cstb])

    def V(i, kc):
        return vecs[:, i * 8 + kc:i * 8 + kc + 1]

    mask4 = cst[:, 0:512]
    maskL = cst[:, 512:512 + NU * 128]
    identf = cst[:, 1024:1152]
    segm = cst[:, 1280:1280 + T]
    identb, identbb = A("rw_identb", (128, 128), BF16)
    P.op("pool", lambda e: e.tensor_copy(out=identb[:], in_=identf), reads=[cstb], writes=[identbb])
    bones, bonesb = A("rw_bones", (128, 128), BF16)
    P.op("pool", lambda e: e.tensor_copy(out=bones[:], in_=cst[:, 1152:1280]), reads=[cstb], writes=[bonesb])
    ident4 = cst[:, 1792:1792 + NU * 128]
    omka, omkab = A("rw_omka", (128, 8), F32)
    P.op("pool", lambda e: e.tensor_scalar(out=omka[:], in0=vecs[:, V_KA * 8:V_KA * 8 + 8], scalar1=-1.0, scalar2=1.0,
                                            op0=ALU.mult, op1=ALU.add), reads=[vecsb], writes=[omkab])
    S_f, _ = A("rw_Sf", (128, KC, 128), F32)
    S_b, _ = A("rw_Sb", (128, KC, 128), BF16)
    Sfb = [P.buf("Sf%d" % c) for c in range(KC)]
    Sbb = [P.buf("Sb%d" % c) for c in range(KC)]
    for c in range(KC):
        P.op("pool", lambda e, c=c: e.memset(S_f[:, c, :], 0.0), writes=[Sfb[c]])
        P.op("pool", lambda e, c=c: e.tensor_copy(out=S_f[0:64, c, 64:128], in_=identf[0:64, 0:64]), reads=[cstb, Sfb[c]], writes=[Sfb[c]])
        P.op("pool", lambda e, c=c: e.tensor_copy(out=S_f[64:128, c, 64:128], in_=identf[64:128, 64:128]), reads=[cstb, Sfb[c]], writes=[Sfb[c]])
        P.op("pool", lambda e, c=c: e.tensor_copy(out=S_b[:, c, :], in_=S_f[:, c, :]), reads=[Sfb[c]], writes=[Sbb[c]])

    h, hb = A("rw_h", (128, KC, T1), F32)
    sq, _ = A("rw_sq", (128, KC, T1), BF16)
    sqb = [P.buf("rwsq%d" % k) for k in range(KC)]
    rstd, rstdb = A("rw_rstd", (128, T1), F32)
    xx, _ = A("rw_xx", (128, KC, T), F32)
    xxb = [P.buf("rwxx%d" % k) for k in range(KC)]
    xi = []
    for i in range(6):
        t_, _ = A("rw_xi%d" % i, (128, KC, T), BF16)
        xi.append((t_, [P.buf("rwxi%d_%d" % (i, k)) for k in range(KC)]))
    t1b_, t1bb = A("rw_t1", (64, T), BF16)
    t2b_, t2bb = A("rw_t2", (64, T), BF16)
    t3a, t3ab = A("rw_t3a", (128, T), BF16)
    t3b, t3bb = A("rw_t3b", (32, T), BF16)
    if has_vres:
        t4b_, t4bb = A("rw_t4", (32, T), BF16)

    PREP_F32 = ("r", "k", "sig", "cs", "a", "e2", "e3", "e4", "kkr", "kkn", "tmp", "kf", "bsc", "g", "bon", "v", "bh", "kh") + (("vf",) if has_vres else ())

    def prep_set(s):
        d = {}
        for nm in PREP_F32:
            d[nm] = A("rw_%s%d" % (nm, s), (128, T), F32)
        d["nb"] = A("rw_nb%d" % s, (128, CH), F32)
        for nm in ("sqk", "rkk"):
            d[nm] = A("rw_%s%d" % (nm, s), (128, T), BF16)
        return d

    def scan_set(s):
        d = {}
        d["e1"] = A("rw_e1%d" % s, (128, T), F32)
        for nm in ("bt", "kt"):
            d[nm] = A("rw_%s%d" % (nm, s), (128, T), BF16)
        d["ar"] = [A("rw_ar%d_%d" % (s, i), (128, CH, 2, 128), BF16) for i in range(2)]
        for i in range(2):
            P.op("pool", lambda e, t_=d["ar"][i][0]: e.memset(t_[:], 0.0), writes=[d["ar"][i][1]])
        d["bhT"] = A("rw_bhT%d" % s, (128, CH, 128), BF16)
        d["khT"] = A("rw_khT%d" % s, (128, CH, 128), BF16)
        d["vpad"] = A("rw_vpad%d" % s, (128, CH, 2, 128), BF16)
        d["Aall"] = A("rw_Aall%d" % s, (128, NU, 512), BF16)
        d["NN"] = [A("rw_NN%d_%d" % (s, i), (128, NU, 128), BF16) for i in range(2)]
        d["LL"] = [A("rw_LL%d_%d" % (s, i), (128, NU, 128), BF16) for i in range(2)]
        d["Tt"] = A("rw_Tt%d" % s, (128, NU, 128), BF16)
        d["Xs"] = A("rw_Xs%d" % s, (128, 256), BF16)
        d["Us"] = A("rw_Us%d" % s, (128, 256), BF16)
        d["yt"] = A("rw_yt%d" % s, (128, 2, T), F32)
        vp = d["vpad"]
        P.op("pool", lambda e, vp=vp: e.memset(vp[0][:], 0.0), writes=[vp[1]])
        return d

    psets = [prep_set(i) for i in range(2)]
    ssets = [scan_set(i) for i in range(KC)]

    hin = hT_in.rearrange("(k p) t -> p k t", p=128)
    stores = []
    fns = {}

    def prologue(it):
        t0 = it * T
        P.dma("sp", h[:], hin[:, :, HALO - 1 + t0:HALO + t0 + T], writes=[hb])
        yield
        for kc in range(KC):
            P.op("pool", lambda e, kc=kc: e.tensor_tensor(out=sq[:, kc, :], in0=h[:, kc, :], in1=h[:, kc, :], op=ALU.mult),
                 reads=[hb], writes=[sqb[kc]])
        for (c0, c1) in ((0, 1), (1, T1)):
            pt, pb = C.bank()
            n = c1 - c0
            for kc in range(KC):
                P.op("pe", lambda e, kc=kc, pt=pt, c0=c0, c1=c1, n=n: e.matmul(pt[:, :n], C.ones[:], sq[:, kc, c0:c1],
                                                                                 start=(kc == 0), stop=(kc == KC - 1)),
                     reads=[sqb[kc], C.ones_b], writes=[pb])
            P.op("act", lambda e, pt=pt, c0=c0, c1=c1, n=n: e.activation(out=rstd[:, c0:c1], in_=pt[:, :n], func=AF.Sqrt,
                                                                          bias=C.eps_t[:, 0:1]),
                 reads=[pb, C.eps_b], writes=[rstdb])
        P.op("dve", lambda e: e.reciprocal(out=rstd[:], in_=rstd[:]), reads=[rstdb], writes=[rstdb])
        for kc in range(KC):
            P.op("dve", lambda e, kc=kc: e.scalar_tensor_tensor(out=h[:, kc, :], in0=h[:, kc, :], scalar=V(V_GAIN, kc),
                                                                in1=rstd[:], op0=ALU.mult, op1=ALU.mult),
                 reads=[hb, rstdb, vecsb], writes=[hb])
        yield
        for kc in range(KC):
            P.op("pool", lambda e, kc=kc: e.tensor_tensor(out=xx[:, kc, :], in0=h[:, kc, 0:T], in1=h[:, kc, 1:T1], op=ALU.subtract),
                 reads=[hb], writes=[xxb[kc]])
        yield
        for i in range(6):
            yield
            for kc in range(KC):
                P.op("dve", lambda e, kc=kc, i=i: e.scalar_tensor_tensor(out=xi[i][0][:, kc, :], in0=xx[:, kc, :], scalar=V(V_MU + i, kc),
                                                                         in1=h[:, kc, 1:T1], op0=ALU.mult, op1=ALU.add),
                     reads=[xxb[kc], hb, vecsb], writes=[xi[i][1][kc]])
        xr, xw, xk, xv, xa, xg = xi
        yield
        pt, pb = C.bank()
        for kc in range(KC):
            P.op("pe", lambda e, kc=kc, pt=pt: e.matmul(pt[0:64, :T], w1[:, kc, :], xw[0][:, kc, :], start=(kc == 0), stop=(kc == KC - 1)),
                 reads=[w1b, xw[1][kc]], writes=[pb])
        P.op("act", lambda e, pt=pt: e.activation(out=t1b_[:], in_=pt[0:64, :T], func=AF.Tanh), reads=[pb], writes=[t1bb])
        pt, pb = C.bank()
        for kc in range(KC):
            P.op("pe", lambda e, kc=kc, pt=pt: e.matmul(pt[0:64, :T], a1[:, kc, :], xa[0][:, kc, :], start=(kc == 0), stop=(kc == KC - 1)),
                 reads=[a1b, xa[1][kc]], writes=[pb])
        P.op("act", lambda e, pt=pt: e.copy(out=t2b_[:], in_=pt[0:64, :T]), reads=[pb], writes=[t2bb])
        pt, pb = C.bank()
        for kc in range(KC):
            P.op("pe", lambda e, kc=kc, pt=pt: e.matmul(pt[:, :T], g1[:, kc, 0:128], xg[0][:, kc, :], start=(kc == 0), stop=(kc == KC - 1)),
                 reads=[g1b, xg[1][kc]], writes=[pb])
        P.op("act", lambda e, pt=pt: e.activation(out=t3a[:], in_=pt[:, :T], func=AF.Sigmoid), reads=[pb], writes=[t3ab])
        pt, pb = C.bank()
        for kc in range(KC):
            P.op("pe", lambda e, kc=kc, pt=pt: e.matmul(pt[0:32, :T], g1[:, kc, 128:160], xg[0][:, kc, :], start=(kc == 0), stop=(kc == KC - 1)),
                 reads=[g1b, xg[1][kc]], writes=[pb])
        P.op("act", lambda e, pt=pt: e.activation(out=t3b[:], in_=pt[0:32, :T], func=AF.Sigmoid), reads=[pb], writes=[t3bb])
        if has_vres:
            pt, pb = C.bank()
            for kc in range(KC):
                P.op("pe", lambda e, kc=kc, pt=pt: e.matmul(pt[0:32, :T], v1[:, kc, :], xv[0][:, kc, :], start=(kc == 0), stop=(kc == KC - 1)),
                     reads=[v1b, xv[1][kc]], writes=[pb])
            P.op("act", lambda e, pt=pt: e.copy(out=t4b_[:], in_=pt[0:32, :T]), reads=[pb], writes=[t4bb])

        def prep_body(c, Wp, Ws):
            W = dict(Wp)
            W.update(Ws)
            cs_ = slice(c * 128, (c + 1) * 128)

            def proj(wt, wtb, x, dst, dstb, eng="act"):
                pt, pb = C.bank()
                for kc in range(KC):
                    P.op("pe", lambda e, kc=kc, pt=pt: e.matmul(pt[:, :T], wt[:, kc, cs_], x[0][:, kc, :], start=(kc == 0), stop=(kc == KC - 1)),
                         reads=[wtb, x[1][kc]], writes=[pb])
                if eng == "act":
                    P.op("act", lambda e, pt=pt: e.copy(out=dst, in_=pt[:, :T]), reads=[pb], writes=[dstb])
                else:
                    P.op("dve", lambda e, pt=pt: e.tensor_copy(out=dst, in_=pt[:, :T]), reads=[pb], writes=[dstb])

            r_, rb_ = W["r"]
            k_, kb_ = W["k"]
            proj(wr, wrb, xr, r_[:], rb_, "act")
            proj(wk, wkb, xk, k_[:], kb_, "dve")
            vfull, vb_ = W["v"]
            v_ = vfull[:]
            proj(wv, wvb, xv, v_, vb_, "act")
            if has_vres:
                vft, vftb = W["vf"]
                P.dma("sp", vft[:], vfirst_d[c * 128:(c + 1) * 128, t0:t0 + T], writes=[vftb])
            sig, sigb = W["sig"]
            pt, pb = C.bank()
            P.op("pe", lambda e, pt=pt: e.matmul(pt[:, :T], w2[0:64, cs_], t1b_[0:64, :], start=True, stop=True), reads=[w2b, t1bb], writes=[pb])
            P.op("act", lambda e, pt=pt: e.activation(out=sig[:], in_=pt[:, :T], func=AF.Sigmoid, bias=V(V_W0, c)),
                 reads=[pb, vecsb], writes=[sigb])
            a_, ab_ = W["a"]
            pt, pb = C.bank()
            P.op("pe", lambda e, pt=pt: e.matmul(pt[:, :T], a2[0:64, cs_], t2b_[0:64, :], start=True, stop=True), reads=[a2b, t2bb], writes=[pb])
            P.op("act", lambda e, pt=pt: e.activation(out=a_[:], in_=pt[:, :T], func=AF.Sigmoid, bias=V(V_A0, c)),
                 reads=[pb, vecsb], writes=[ab_])
            pt, pb = C.bank()
            P.op("pe", lambda e, pt=pt: e.matmul(pt[:, :T], g2a[:, cs_], t3a[:], start=True, stop=False), reads=[g2ab, t3ab], writes=[pb])
            P.op("pe", lambda e, pt=pt: e.matmul(pt[:, :T], g2b[0:32, cs_], t3b[0:32, :], start=False, stop=True), reads=[g2bb, t3bb], writes=[pb])
            gg, ggb = W["g"]
            P.op("act", lambda e, pt=pt: e.copy(out=gg[:], in_=pt[:, :T]), reads=[pb], writes=[ggb])
            stores.append(P.dma("sp", g_d[c * 128:(c + 1) * 128, t0:t0 + T], gg[:], reads=[ggb]))
            if has_vres:
                tmp, tmpb = W["tmp"]
                pt, pb = C.bank()
                P.op("pe", lambda e, pt=pt: e.matmul(pt[:, :T], v2[0:32, cs_], t4b_[0:32, :], start=True, stop=True), reads=[v2b, t4bb], writes=[pb])
                P.op("act", lambda e, pt=pt: e.activation(out=tmp[:], in_=pt[:, :T], func=AF.Sigmoid, bias=V(V_V0, c)),
                     reads=[pb, vecsb], writes=[tmpb])
                e1, e1b = W["e1"]
                P.op("pool", lambda e: e.tensor_tensor(out=e1[:], in0=vft[:], in1=v_, op=ALU.subtract), reads=[vftb, vb_], writes=[e1b])
                P.op("pool", lambda e: e.tensor_tensor(out=e1[:], in0=e1[:], in1=tmp[:], op=ALU.mult), reads=[e1b, tmpb], writes=[e1b])
                P.op("pool", lambda e: e.tensor_tensor(out=v_, in0=v_, in1=e1[:], op=ALU.add), reads=[e1b, vb_], writes=[vb_])
            yield
            cs, csb = W["cs"]
            P.op("dve", lambda e: e.tensor_tensor_scan(out=cs[:], data0=segm, data1=sig[:], initial=0.0, op0=ALU.mult, op1=ALU.add),
                 reads=[cstb, sigb], writes=[csb])
            e1, e1b = W["e1"]
            e2, e2b = W["e2"]
            e3, e3b = W["e3"]
            e4, e4b = W["e4"]
            nb, nbb = W["nb"]
            P.op("act", lambda e: e.activation(out=e1[:], in_=cs[:], func=AF.Exp, scale=-KAPPA), reads=[csb], writes=[e1b])
            tmp, tmpb = W["tmp"]
            P.op("pool", lambda e: e.tensor_tensor(out=tmp[:], in0=cs[:], in1=sig[:], op=ALU.subtract), reads=[csb, sigb], writes=[tmpb])
            P.op("act", lambda e: e.activation(out=e2[:], in_=tmp[:], func=AF.Exp, scale=-KAPPA), reads=[tmpb], writes=[e2b])
            P.op("act", lambda e: e.activation(out=e3[:], in_=cs[:], func=AF.Exp, scale=KAPPA), reads=[csb], writes=[e3b])
            for ch in range(CH):
                P.op("pool", lambda e, ch=ch: e.tensor_scalar(out=nb[:, ch:ch + 1], in0=cs[:, ch * 128 + 127:ch * 128 + 128], scalar1=-KAPPA,
                                                              scalar2=None, op0=ALU.mult), reads=[csb], writes=[nbb])
            for ch in range(CH):
                P.op("act", lambda e, ch=ch: e.activation(out=e4[:, ch * 128:(ch + 1) * 128], in_=cs[:, ch * 128:(ch + 1) * 128], func=AF.Exp,
                                                          scale=KAPPA, bias=nb[:, ch:ch + 1]), reads=[csb, nbb], writes=[e4b])
            yield
            kkr, kkrb = W["kkr"]
            kkn, kknb = W["kkn"]
            sqk, sqkb = W["sqk"]
            P.op("act", lambda e: e.activation(out=kkr[:], in_=k_[:], func=AF.Identity, scale=V(V_KK, c)),
                 reads=[kb_, vecsb], writes=[kkrb])
            P.op("act", lambda e: e.activation(out=sqk[:], in_=k_[:], func=AF.Square, scale=V(V_KK, c)),
                 reads=[kb_, vecsb], writes=[sqkb])
            pt, pb = C.bank()
            P.op("pe", lambda e, pt=pt: e.matmul(pt[:, :T], bones[:], sqk[:], start=True, stop=True), reads=[bonesb, sqkb], writes=[pb])
            P.op("act", lambda e, pt=pt: e.activation(out=kkn[:], in_=pt[:, :T], func=AF.Sqrt), reads=[pb], writes=[kknb])
            P.op("dve", lambda e: e.tensor_scalar(out=kkn[:], in0=kkn[:], scalar1=1e-12, scalar2=None, op0=ALU.max), reads=[kknb], writes=[kknb])
            P.op("dve", lambda e: e.reciprocal(out=kkn[:], in_=kkn[:]), reads=[kknb], writes=[kknb])
            P.op("pool", lambda e: e.tensor_tensor(out=kkn[:], in0=kkn[:], in1=kkr[:], op=ALU.mult), reads=[kknb, kkrb], writes=[kknb])
            yield
            kf, kfb = W["kf"]
            P.op("act", lambda e: e.activation(out=kf[:], in_=a_[:], func=AF.Identity, scale=V(V_KA, c), bias=omka[:, c:c + 1]),
                 reads=[ab_, vecsb, omkab], writes=[kfb])
            P.op("pool", lambda e: e.tensor_tensor(out=kf[:], in0=kf[:], in1=k_[:], op=ALU.mult), reads=[kfb, kb_], writes=[kfb])
            rkk, rkkb = W["rkk"]
            P.op("dve", lambda e: e.scalar_tensor_tensor(out=rkk[:], in0=r_[:], scalar=V(V_RK, c), in1=kf[:], op0=ALU.mult, op1=ALU.mult),
                 reads=[rb_, kfb, vecsb], writes=[rkkb])
            pt, pb = C.bank()
            P.op("pe", lambda e, pt=pt: e.matmul(pt[:, :T], bones[:], rkk[:], start=True, stop=True), reads=[bonesb, rkkb], writes=[pb])
            bon, bonb = W["bon"]
            P.op("dve", lambda e, pt=pt: e.tensor_tensor(out=bon[:], in0=pt[:, :T], in1=v_, op=ALU.mult), reads=[pb, vb_], writes=[bonb])
            stores.append(P.dma("sp", bonus_d[c * 128:(c + 1) * 128, t0:t0 + T], bon[:], reads=[bonb]))
            if not has_vres:
                stores.append(P.dma("sp", v_d[c * 128:(c + 1) * 128, t0:t0 + T], v_, reads=[vb_]))
            yield
            arX = W["ar"]
            bsc, bscb = W["bsc"]
            bt, btb = W["bt"]
            kt, ktb = W["kt"]
            bh, bhb = W["bh"]
            kh, khb = W["kh"]

            def v3(ap):
                return ap.rearrange("p (c t) -> p c t", t=128)

            for hh in range(2):
                hs = slice(hh * 64, (hh + 1) * 64)
                arh, arhb = arX[hh]
                P.op("dve", lambda e, hs=hs, arh=arh: e.scalar_tensor_tensor(out=arh[hs, :, 0, :], in0=v3(kkn[hs, :]), scalar=-1.0, in1=v3(e2[hs, :]),
                                                                             op0=ALU.mult, op1=ALU.mult),
                     reads=[kknb, e2b, arhb], writes=[arhb])
                P.op("pool", lambda e, hs=hs, arh=arh: e.tensor_tensor(out=arh[hs, :, 1, :], in0=v3(r_[hs, :]), in1=v3(e1[hs, :]), op=ALU.mult),
                     reads=[rb_, e1b, arhb], writes=[arhb])
            P.op("pool", lambda e: e.tensor_tensor(out=bsc[:], in0=kkn[:], in1=a_[:], op=ALU.mult), reads=[kknb, ab_], writes=[bscb])
            P.op("pool", lambda e: e.tensor_tensor(out=bt[:], in0=bsc[:], in1=e3[:], op=ALU.mult), reads=[bscb, e3b], writes=[btb])
            P.op("dve", lambda e: e.tensor_tensor(out=bh[:], in0=bsc[:], in1=e4[:], op=ALU.mult), reads=[bscb, e4b], writes=[bhb])
            P.op("pool", lambda e: e.tensor_tensor(out=kt[:], in0=kf[:], in1=e3[:], op=ALU.mult), reads=[kfb, e3b], writes=[ktb])
            P.op("dve", lambda e: e.tensor_tensor(out=kh[:], in0=kf[:], in1=e4[:], op=ALU.mult), reads=[kfb, e4b], writes=[khb])
            yield
            bhT, bhTb = W["bhT"]
            khT, khTb = W["khT"]
            vpad, vpadb = W["vpad"]
            for si, (src, srcb) in enumerate(((bh, bhb), (kh, khb), (None, vb_))):
                ptf, ptb = C.bank()
                for ch in range(CH):
                    src_ap = v_[:, ch * 128:(ch + 1) * 128] if src is None else src[:, ch * 128:(ch + 1) * 128]
                    P.op("pe", lambda e, src_ap=src_ap, ch=ch, ptf=ptf: e.transpose(ptf[:, ch * 128:(ch + 1) * 128], src_ap, identf),
                         reads=[srcb, cstb], writes=[ptb])
                if si == 0:
                    P.op("act", lambda e, ptf=ptf: e.copy(out=bhT[:].rearrange("p c t -> p (c t)"), in_=ptf[:, 0:CH * 128]), reads=[ptb], writes=[bhTb])
                elif si == 1:
                    P.op("dve", lambda e, ptf=ptf: e.tensor_copy(out=khT[:].rearrange("p c t -> p (c t)"), in_=ptf[:, 0:CH * 128]), reads=[ptb], writes=[khTb])
                else:
                    for ch in range(CH):
                        for hh in range(2):
                            P.op("act",
                                 lambda e, ptf=ptf, ch=ch, hh=hh: e.copy(
                                     out=vpad[:, ch, hh, 0:64], in_=ptf[:, ch * 128 + hh * 64:ch * 128 + (hh + 1) * 64]),
                                 reads=[ptb, vpadb], writes=[vpadb])
            yield
            Aall, Aallb = W["Aall"]
            bankL, bankLb = C.bank()
            import os
            DBG = os.environ.get("RW_DBG", "")
            for ch in range(CH):
                for hh in range(2):
                    if "nohh1" in DBG and hh == 1:
                        continue
                    u = ch * 2 + hh
                    hs = slice(hh * 64, (hh + 1) * 64)
                    pa, pab = C.bank()
                    ar, arb = arX[hh]
                    arflat = ar[:, ch, :, :].rearrange("p a t -> p (a t)")
                    P.op("pe", lambda e, pa=pa, ch=ch, arflat=arflat: e.matmul(pa[:, 0:256], bt[:, ch * 128:(ch + 1) * 128], arflat, start=True, stop=True),
                         reads=[btb, arb], writes=[pab])
                    P.op("pe", lambda e, pa=pa, ch=ch, arflat=arflat: e.matmul(pa[:, 256:512], kt[:, ch * 128:(ch + 1) * 128], arflat, start=True, stop=True),
                         reads=[ktb, arb], writes=[pab])
                    P.op("dve", lambda e, pa=pa, u=u: e.tensor_tensor(out=Aall[:, u, :], in0=pa[:, :], in1=mask4, op=ALU.mult),
                         reads=[pab, cstb, Aallb], writes=[Aallb])
                    P.op("pe", lambda e, ar=ar, ch=ch, u=u: e.matmul(bankL[:, u * 128:(u + 1) * 128], ar[:, ch, 0, :], bt[:, ch * 128:(ch + 1) * 128], start=True, stop=True),
                         reads=[arb, btb], writes=[bankLb])
            NN = W["NN"]
            LL = W["LL"]
            Tt, Ttb = W["Tt"]
            L0, L0b = LL[0]
            if "noL0" not in DBG:
                P.op("dve", lambda e: e.tensor_tensor(out=L0[:].rearrange("p u t -> p (u t)"), in0=bankL[:, 0:NU * 128], in1=maskL, op=ALU.mult),
                     reads=[bankLb, cstb], writes=[L0b])
            if "noTt" not in DBG:
              P.op("pool", lambda e: e.tensor_tensor(out=Tt[:], in0=Aall[:, :, 0:128], in1=ident4.rearrange("p (u t) -> p u t", t=128), op=ALU.add),
                 reads=[Aallb, cstb], writes=[Ttb])
            return

        def scan_body(c, W):
            e1, e1b = W["e1"]
            arX = W["ar"]
            bhT, bhTb = W["bhT"]
            khT, khTb = W["khT"]
            vpad, vpadb = W["vpad"]
            Aall, Aallb = W["Aall"]
            NN = W["NN"]
            LL = W["LL"]
            Tt, Ttb = W["Tt"]
            L0, L0b = LL[0]
            Nprev = None
            Lprev, Lprevb = L0, L0b
            for lv in range(1, 7):
                Lcur, Lcurb = LL[lv % 2]
                bN, bNb = C.bank()
                bL, bLb = C.bank()
                for u in range(NU):
                    us = slice(u * 128, (u + 1) * 128)
                    nprev_ap = Aall[:, u, 0:128] if Nprev is None else Nprev[0][:, u, :]
                    nprev_b = Aallb if Nprev is None else Nprev[1]
                    if lv < 6:
                        P.op("pe", lambda e, u=u, us=us, nprev_ap=nprev_ap, Lprev=Lprev: e.matmul(bN[:, us], Lprev[:, u, :], nprev_ap, start=True, stop=True),
                             reads=[Lprevb, nprev_b], writes=[bNb])
                    P.op("pe", lambda e, u=u, us=us, nprev_ap=nprev_ap, Lprev=Lprev: e.matmul(bL[:, us], nprev_ap, Lprev[:, u, :], start=True, stop=True),
                         reads=[Lprevb, nprev_b], writes=[bLb])
                if lv < 6:
                    Ncur, Ncurb = NN[lv % 2]
                    P.op("act", lambda e, Ncur=Ncur, bN=bN: e.copy(out=Ncur[:].rearrange("p u t -> p (u t)"), in_=bN[:, 0:NU * 128]), reads=[bNb], writes=[Ncurb])
                    Nprev = (Ncur, Ncurb)
                P.op("act", lambda e, Lcur=Lcur, bL=bL: e.copy(out=Lcur[:].rearrange("p u t -> p (u t)"), in_=bL[:, 0:NU * 128]), reads=[bLb], writes=[Lcurb])
                yield
                bP, bPb = C.bank()
                for u in range(NU):
                    us = slice(u * 128, (u + 1) * 128)
                    P.op("pe", lambda e, u=u, us=us, Lcur=Lcur: e.matmul(bP[:, us], Lcur[:, u, :], Tt[:, u, :], start=True, stop=True),
                         reads=[Lcurb, Ttb], writes=[bPb])
                P.op("dve", lambda e, bP=bP: e.tensor_tensor(out=Tt[:].rearrange("p u t -> p (u t)"), in0=bP[:, 0:NU * 128],
                                                             in1=Tt[:].rearrange("p u t -> p (u t)"), op=ALU.add), reads=[bPb, Ttb], writes=[Ttb])
                Lprev, Lprevb = Lcur, Lcurb
                yield
            yield
            Xs, Xsb = W["Xs"]
            Us, Usb = W["Us"]
            yt, ytb = W["yt"]
            for ch in range(CH):
                bX, bXb = C.bank()
                for hh in range(2):
                    u = ch * 2 + hh
                    hs = slice(hh * 64, (hh + 1) * 64)
                    hc_ = slice(hh * 128, (hh + 1) * 128)
                    P.op("pe", lambda e, u=u, hc_=hc_, ch=ch, hh=hh: e.matmul(bX[:, hc_], Aall[:, u, 256:384], vpad[:, ch, hh, :], start=True, stop=False),
                         reads=[Aallb, vpadb], writes=[bXb])
                    P.op("pe", lambda e, hh=hh, hc_=hc_, ch=ch: e.matmul(bX[:, hc_], arX[hh][0][:, ch, 0, :], S_b[:, c, :], start=False, stop=True),
                         reads=[arX[hh][1], Sbb[c]], writes=[bXb])
                P.op("act", lambda e, bX=bX: e.copy(out=Xs[:], in_=bX[:, 0:256]), reads=[bXb], writes=[Xsb])
                yield
                bU, bUb = C.bank()
                for hh in range(2):
                    u = ch * 2 + hh
                    hc_ = slice(hh * 128, (hh + 1) * 128)
                    P.op("pe", lambda e, u=u, hc_=hc_: e.matmul(bU[:, hc_], Tt[:, u, :], Xs[:, hc_], start=True, stop=True),
                         reads=[Ttb, Xsb], writes=[bUb])
                P.op("dve", lambda e, bU=bU: e.tensor_copy(out=Us[:], in_=bU[:, 0:256]), reads=[bUb], writes=[Usb])
                yield
                bY, bYb = C.bank()
                for hh in range(2):
                    u = ch * 2 + hh
                    hs = slice(hh * 64, (hh + 1) * 64)
                    hc_ = slice(hh * 128, (hh + 1) * 128)
                    P.op("pe", lambda e, hh=hh, hc_=hc_, ch=ch: e.matmul(bY[:, hc_], S_b[:, c, :], arX[hh][0][:, ch, 1, :], start=True, stop=False),
                         reads=[Sbb[c], arX[hh][1]], writes=[bYb])
                    P.op("pe", lambda e, u=u, hc_=hc_: e.matmul(bY[:, hc_], Us[:, hc_], Aall[:, u, 128:256], start=False, stop=False),
                         reads=[Usb, Aallb], writes=[bYb])
                    P.op("pe", lambda e, u=u, hc_=hc_, ch=ch, hh=hh: e.matmul(bY[:, hc_], vpad[:, ch, hh, :], Aall[:, u, 384:512], start=False, stop=True),
                         reads=[vpadb, Aallb], writes=[bYb])
                P.op("act", lambda e, bY=bY, ch=ch: e.copy(out=yt[:, :, ch * 128:(ch + 1) * 128], in_=bY[:, 0:256].rearrange("p (h t) -> p h t", h=2)),
                     reads=[bYb, ytb], writes=[ytb])
                yield
                bS, bSb = C.bank()
                P.op("pe", lambda e, ch=ch: e.matmul(bS[:, 0:256], bhT[:, ch, :], Us[:], start=True, stop=False), reads=[bhTb, Usb], writes=[bSb])
                P.op("pe", lambda e, ch=ch: e.matmul(bS[:, 0:256], khT[:, ch, :], vpad[:, ch, :, :].rearrange("p h v -> p (h v)"), start=False, stop=True),
                     reads=[khTb, vpadb], writes=[bSb])
                for hh in range(2):
                    hs = slice(hh * 64, (hh + 1) * 64)
                    hc_ = slice(hh * 128, (hh + 1) * 128)
                    P.op("dve", lambda e, hs=hs, hc_=hc_, ch=ch, bS=bS: e.scalar_tensor_tensor(
                        out=S_f[hs, c, :], in0=S_f[hs, c, :], scalar=e1[hs, ch * 128 + 127:ch * 128 + 128], in1=bS[hs, hc_], op0=ALU.mult, op1=ALU.add),
                        reads=[Sfb[c], e1b, bSb], writes=[Sfb[c]])
                P.op("act", lambda e: e.copy(out=S_b[:, c, :], in_=S_f[:, c, :]), reads=[Sfb[c]], writes=[Sbb[c]])
            stores.append(P.dma("sp", y0_d[c * 128:(c + 1) * 128, t0:t0 + T].rearrange("(h v) t -> v h t", h=2), yt[0:64, :, :], reads=[ytb]))
            stores.append(P.dma("sp", z_d[c * 128:(c + 1) * 128, t0:t0 + T].rearrange("(h v) t -> v h t", h=2), yt[64:128, :, :], reads=[ytb]))
        fns[it] = (prep_body, scan_body)

    scans = []

    def tile_driver(it):
        yield from prologue(it)
        prep_body, scan_body = fns[it]
        nextp = 0
        running = []
        while nextp < KC or running:
            while len(running) < 2 and nextp < KC:
                running.append((nextp, prep_body(nextp, psets[nextp % 2], ssets[nextp])))
                nextp += 1
            for item in list(running):
                c_, gen = item
                try:
                    next(gen)
                except StopIteration:
                    running.remove(item)
                    scans.append(scan_body(c_, ssets[c_]))
            yield

    def step_scans():
        for gn_ in list(scans):
            try:
                next(gn_)
            except StopIteration:
                scans.remove(gn_)

    for it in range(NT):
        drv = tile_driver(it)
        while True:
            try:
                next(drv)
            except StopIteration:
                break
            step_scans()
    while scans:
        step_scans()
    stores.append(P.dma("sp", state_d, S_f[:], reads=Sfb, semkey=Sfb[0]))
    return stores


GN_EPS = 64e-5


def phase_rwkv_post(P, C, hT_in, hT_out, y0_d, z_d, bonus_d, g_d, states_d, mvec_d, w_o, vecs_d, cst_d, NT, T):
    A = P.alloc
    wo, wob = A("po_wo", (128, KC, D), BF16)
    P.dma("pool", wo[:], w_o.rearrange("(k p) n -> p k n", p=128), writes=[wob])
    vecs, vecsb = A("po_vecs", (128, NVEC * 8), F32)
    P.dma("sp", vecs[:], vecs_d, writes=[vecsb])
    cst, cstb = A("po_cst", (128, CST_W), F32)
    P.dma("sp", cst[:], cst_d, writes=[cstb])
    identf = cst[:, 1024:1152]
    bonesf = cst[:, 1152:1280]
    G, Gb = A("po_G", (128, 3, KC, 128), F32)
    P.dma("sp", G[:], states_d.rearrange("j p c x -> p j c x"), writes=[Gb])
    mv, mvb = A("po_mv", (128, 6), F32)
    P.dma("sp", mv[:], mvec_d, writes=[mvb])
    gne, gneb = A("po_gne", (128, 1), F32)
    P.op("pool", lambda e: e.memset(gne[:], GN_EPS), writes=[gneb])
    SS = [A("po_SS%d" % c, (128, 128), F32) for c in range(KC)]
    for c in range(KC):
        P.op("pool", lambda e, c=c: e.memset(SS[c][0][:], 0.0), writes=[SS[c][1]])
    BD = [A("po_BD%d" % i, (128, 128), F32) for i in range(2)]
    QB = [A("po_QB%d" % i, (128, 128), F32) for i in range(2)]
    PB = [A("po_PB%d" % i, (128, 128), F32) for i in range(2)]
    for i in range(2):
        P.op("pool", lambda e, i=i: e.memset(BD[i][0][:], 0.0), writes=[BD[i][1]])
        P.op("pool", lambda e, i=i: e.memset(QB[i][0][:], 0.0), writes=[QB[i][1]])
    n = 0
    for j in range(3):
        for c in range(KC):
            bd, bdb = BD[n % 2]
            qb, qbb = QB[n % 2]
            pbt, pbb = PB[n % 2]
            n += 1
            for hh in range(2):
                hs = slice(hh * 64, (hh + 1) * 64)
                P.op("pool", lambda e, hs=hs, j=j, c=c, bd=bd: e.tensor_scalar(out=bd[hs, hs], in0=G[hs, j, c, 64:128], scalar1=mv[hs, j:j + 1], scalar2=None, op0=ALU.mult),
                     reads=[Gb, mvb, bdb], writes=[bdb])
                P.op("pool", lambda e, hs=hs, j=j, c=c, qb=qb: e.tensor_scalar(out=qb[hs, hs], in0=G[hs, j, c, 0:64], scalar1=mv[hs, j:j + 1], scalar2=None, op0=ALU.mult),
                     reads=[Gb, mvb, qbb], writes=[qbb])
            P.op("dve", lambda e, j=j, bd=bd: e.scalar_tensor_tensor(out=bd[:], in0=identf, scalar=mv[:, 3 + j:4 + j], in1=bd[:], op0=ALU.mult, op1=ALU.add),
                 reads=[cstb, mvb, bdb], writes=[bdb])
            pt, pb = C.bank()
            P.op("pe", lambda e, pt=pt, bd=bd: e.transpose(pt[:, 0:128], bd[:], identf), reads=[bdb, cstb], writes=[pb])
            P.op("act", lambda e, pt=pt, pbt=pbt: e.copy(out=pbt[:], in_=pt[:, 0:128]), reads=[pb], writes=[pbb])
            pt2, pb2 = C.bank()
            P.op("pe", lambda e, pt2=pt2, pbt=pbt, c=c: e.matmul(pt2[:, 0:128], pbt[:], SS[c][0][:], start=True, stop=True), reads=[pbb, SS[c][1]], writes=[pb2])
            P.op("dve", lambda e, pt2=pt2, qb=qb, c=c: e.tensor_tensor(out=SS[c][0][:], in0=pt2[:, 0:128], in1=qb[:], op=ALU.add), reads=[pb2, qbb], writes=[SS[c][1]])

    def V(i, kc):
        return vecs[:, i * 8 + kc:i * 8 + kc + 1]

    tiles = [[A("po_%s%d" % (nm, i), (128, KC, T), F32) for nm in ("h", "bo", "g", "y0", "z")] for i in range(2)]
    yg, _ = A("po_yg", (128, KC, T), BF16)
    ygb = [P.buf("poyg%d" % k) for k in range(KC)]
    sc = [[A("po_s%d_%d" % (i, k), (128, T), F32) for k in range(3)] for i in range(KC)]
    hin = hT_in.rearrange("(k p) t -> p k t", p=128)
    hout = hT_out.rearrange("(k p) t -> p k t", p=128)

    def fm(d):
        return d.rearrange("(k p) t -> p k t", p=128)

    stores = []
    for it in range(NT):
        t0 = it * T
        (h, hb), (bo, bob), (gg, ggb), (y0, y0b), (zz, zzb) = tiles[it % 2]
        P.dma("sp", h[:], hin[:, :, HALO + t0:HALO + t0 + T], writes=[hb])
        P.dma("sp", bo[:], fm(bonus_d)[:, :, t0:t0 + T], writes=[bob])
        P.dma("sp", gg[:], fm(g_d)[:, :, t0:t0 + T], writes=[ggb])
        P.dma("sp", y0[:], fm(y0_d)[:, :, t0:t0 + T], writes=[y0b])
        P.dma("sp", zz[:], fm(z_d)[:, :, t0:t0 + T], writes=[zzb])
        def chain(c):
            (y, yb), (d, db), (q, qb_) = sc[c]
            yield
            pt, pb = C.bank()
            P.op("pe", lambda e, pt=pt, c=c: e.matmul(pt[:, :T], SS[c][0][:], zz[:, c, :], start=True, stop=True), reads=[SS[c][1], zzb], writes=[pb])
            yield
            P.op("dve", lambda e, pt=pt, c=c, y=y: e.tensor_tensor(out=y[:], in0=pt[:, :T], in1=y0[:, c, :], op=ALU.add), reads=[pb, y0b], writes=[yb])
            yield
            pt, pb = C.bank()
            P.op("pe", lambda e, pt=pt, y=y: e.matmul(pt[:, :T], bonesf, y[:], start=True, stop=True), reads=[cstb, yb], writes=[pb])
            yield
            P.op("dve", lambda e, pt=pt, y=y, d=d: e.scalar_tensor_tensor(out=d[:], in0=pt[:, :T], scalar=-1.0 / 64, in1=y[:], op0=ALU.mult, op1=ALU.add),
                 reads=[pb, yb], writes=[db])
            P.op("act", lambda e, d=d, q=q: e.activation(out=q[:], in_=d[:], func=AF.Square), reads=[db], writes=[qb_])
            yield
            pt, pb = C.bank()
            P.op("pe", lambda e, pt=pt, q=q: e.matmul(pt[:, :T], bonesf, q[:], start=True, stop=True), reads=[cstb, qb_], writes=[pb])
            yield
            P.op("act", lambda e, pt=pt, q=q: e.activation(out=q[:], in_=pt[:, :T], func=AF.Sqrt, bias=gne[:, 0:1], scale=1.0 / 64), reads=[pb, gneb], writes=[qb_])
            yield
            P.op("dve", lambda e, q=q: e.reciprocal(out=q[:], in_=q[:]), reads=[qb_], writes=[qb_])
            yield
            P.op("dve", lambda e, d=d, q=q: e.tensor_tensor(out=d[:], in0=d[:], in1=q[:], op=ALU.mult), reads=[db, qb_], writes=[db])
            yield
            P.op("act", lambda e, d=d, c=c: e.activation(out=d[:], in_=d[:], func=AF.Identity, scale=V(V_LNW, c), bias=V(V_LNB, c)),
                 reads=[db, vecsb], writes=[db])
            yield
            P.op("pool", lambda e, d=d, c=c: e.tensor_tensor(out=d[:], in0=d[:], in1=bo[:, c, :], op=ALU.add), reads=[db, bob], writes=[db])
            yield
            P.op("dve", lambda e, d=d, c=c: e.tensor_tensor(out=yg[:, c, :], in0=d[:], in1=gg[:, c, :], op=ALU.mult), reads=[db, ggb], writes=[ygb[c]])
        gens = [chain(c) for c in range(KC)]
        while gens:
            for gn_ in list(gens):
                try:
                    next(gn_)
                except StopIteration:
                    gens.remove(gn_)
        for dc in range(KC):
            po, pob = C.bank()
            for c in range(KC):
                P.op("pe", lambda e, c=c, dc=dc, po=po: e.matmul(po[:, :T], wo[:, c, dc * 128:(dc + 1) * 128], yg[:, c, :], start=(c == 0), stop=(c == KC - 1)),
                     reads=[wob, ygb[c]], writes=[pob])
            P.op("dve", lambda e, dc=dc, po=po: e.tensor_tensor(out=h[:, dc, :], in0=po[:, :T], in1=h[:, dc, :], op=ALU.add), reads=[pob, hb], writes=[hb])
        stores.append(P.dma("sp", hout[:, :, t0:t0 + T], h[:], reads=[hb]))
    return stores


def _mk(nc):
    def din(name, shape):
        return nc.dram_tensor(name, list(shape), F32, kind="ExternalInput").ap()

    def dout(name, shape):
        return nc.dram_tensor(name, list(shape), F32, kind="ExternalOutput").ap()

    def dint(name, shape):
        return nc.dram_tensor(name, list(shape), F32).ap()
    return din, dout, dint


def _ffn_inputs(din):
    return din("wg", [D, FH]), din("wu", [D, FH]), din("wd", [FH, D]), din("gn", [D])


def build_even(NTOK, final):
    nc = bass.Bass("TRN2", target_bir_lowering=False)
    din, dout, dint = _mk(nc)
    T = 512
    hT = din("hT", [D, HALO + NTOK])
    w_in = din("w_in", [D, 1536]); w_out = din("w_out", [D, D]); mg = din("mg", [D])
    sg = din("sg", [1, 512]); wsT = din("wsT", [128, 4, 128]); sbias = din("sbias", [1, 512]); pw = din("pw", [128, 4, 128])
    psc = din("psc", [512]); invc = din("invc", [128, 4, HALO])
    wg, wu, wd, gn = _ffn_inputs(din)
    fn = din("fn", [D]) if final else None
    hA = dint("hA", [D, NTOK])
    hO = dout("hO", [D, NTOK])
    with ExitStack() as stack:
        P = Prog(nc, stack)
        C = Ctx(P, T)
        P.persist()
        block = stack.enter_context(nc.Block())
        phase_ab(P, C, hT, hA, w_in, w_out, mg, sg, wsT, sbias, pw, psc, invc, NTOK // T)
        P.phase_reset()
        if final:
            stores = phase_ffn(P, C, hA, None, wg, wu, wd, gn, NTOK // T, final_gain=fn, outT=hO)
        else:
            stores = phase_ffn(P, C, hA, hO, wg, wu, wd, gn, NTOK // T)
        P.emit(block, final_waits=stores)
    return nc


_RW_NAMES = ("w_r", "w_k", "w_v", "w1", "w2", "a1", "a2", "g1", "g2", "v1", "v2")
_RW_SHAPES = {"w_r": [D, D], "w_k": [D, D], "w_v": [D, D], "w1": [D, 64], "w2": [64, D], "a1": [D, 64], "a2": [64, D],
              "g1": [D, 160], "g2": [160, D], "v1": [D, 32], "v2": [32, D]}


def build_odd_pre(NTOK, has_vres):
    nc = bass.Bass("TRN2", target_bir_lowering=False)
    din, dout, dint = _mk(nc)
    T = 256
    hT = din("hT", [D, HALO + NTOK])
    Wd = {n: din(n, _RW_SHAPES[n]) for n in _RW_NAMES if has_vres or n not in ("v1", "v2")}
    vecs = din("vecs", [128, NVEC * 8]); cst = din("cst", [128, CST_W])
    vfirst = din("vfirst", [D, NTOK]) if has_vres else None
    y0 = dout("y0", [D, NTOK]); z = dout("z", [D, NTOK]); bonus = dout("bonus", [D, NTOK]); g = dout("g", [D, NTOK])
    v = dout("v", [D, NTOK]) if not has_vres else None
    state = dout("state", [128, 8, 128])
    with ExitStack() as stack:
        P = Prog(nc, stack)
        C = Ctx(P, 512)
        P.persist()
        block = stack.enter_context(nc.Block())
        stores = phase_rwkv_pre(P, C, hT, Wd, vecs, cst, NTOK // T, T, y0, z, bonus, g, v, vfirst, state, has_vres)
        P.emit(block, final_waits=stores)
    return nc


def build_odd_post(NTOK, final):
    nc = bass.Bass("TRN2", target_bir_lowering=False)
    din, dout, dint = _mk(nc)
    T = 512
    hT = din("hT", [D, HALO + NTOK])
    y0 = din("y0", [D, NTOK]); z = din("z", [D, NTOK]); bonus = din("bonus", [D, NTOK]); g = din("g", [D, NTOK])
    states = din("states", [3, 128, 8, 128]); mvec = din("mvec", [128, 6])
    w_o = din("w_o", [D, D]); vecs = din("vecs", [128, NVEC * 8]); cst = din("cst", [128, CST_W])
    wg, wu, wd, gn = _ffn_inputs(din)
    fn = din("fn", [D]) if final else None
    hP = dint("hP", [D, NTOK])
    hO = dout("hO", [D, NTOK])
    with ExitStack() as stack:
        P = Prog(nc, stack)
        C = Ctx(P, T)
        P.persist()
        block = stack.enter_context(nc.Block())
        phase_rwkv_post(P, C, hT, hP, y0, z, bonus, g, states, mvec, w_o, vecs, cst, NTOK // T, T)
        P.phase_reset()
        if final:
            stores = phase_ffn(P, C, hP, None, wg, wu, wd, gn, NTOK // T, final_gain=fn, outT=hO)
        else:
            stores = phase_ffn(P, C, hP, hO, wg, wu, wd, gn, NTOK // T)
        P.emit(block, final_waits=stores)
    return nc


def _pack_vecs(inp, i, layer):
    vs = [inp["mix_norm"][layer]] + [inp["rwkv_mu"][i][j] for j in range(6)] + [
        inp["rwkv_w0"][i], inp["rwkv_a0"][i], inp["rwkv_k_k"][i], inp["rwkv_k_a"][i], np.asarray(inp["rwkv_r_k"][i]).reshape(-1),
        inp["rwkv_v0"][i - 1] if i > 0 else np.zeros(D, np.float32), inp["rwkv_ln_w"][i], inp["rwkv_ln_b"][i]]
    out = np.zeros((128, NVEC * 8), np.float32)
    for j, v in enumerate(vs):
        out[:, j * 8:(j + 1) * 8] = np.asarray(v, np.float32).reshape(8, 128).T
    return out


def _invc_table(first):
    t = np.zeros((128, 4, HALO), np.float32)
    for g, w in enumerate((2, 4, 8, 16)):
        for i in range(HALO):
            t[:, g, i] = 1.0 / (min(i + 1, w) if first else w)
    return t


def _with_halo(hT_list, G):
    out = []
    for c, h in enumerate(hT_list):
        buf = np.zeros((D, HALO + h.shape[1]), np.float32)
        buf[:, HALO:] = h
        if c % G != 0:
            buf[:, :HALO] = hT_list[c - 1][:, -HALO:]
        out.append(buf)
    return out


def run_network(inp, B, G, NTOK, depth=4):
    NC = B * G
    cores = list(range(NC))
    inp = {k: np.asarray(v, np.float32) for k, v in inp.items()}
    x = inp["x"]
    hT = [np.ascontiguousarray(x[c // G, (c % G) * NTOK:(c % G + 1) * NTOK].T) for c in cores]
    cst = make_cst()
    vfirst = None
    cache = {}

    def launch(key, builder, maps):
        if key not in cache:
            cache[key] = builder()
        return run_bass_kernel_spmd(cache[key], maps, core_ids=cores).results

    def ffn_w(layer):
        return {"wg": inp["ffn_w_gate"][layer], "wu": inp["ffn_w_up"][layer], "wd": inp["ffn_w_down"][layer], "gn": inp["ffn_norm"][layer]}

    for layer in range(depth):
        i = layer // 2
        final = layer == depth - 1
        hh = _with_halo(hT, G)
        if layer % 2 == 0:
            common = {"w_in": inp["ab_w_in"][i], "w_out": inp["ab_w_out"][i], "mg": inp["mix_norm"][layer], "sg": inp["sgu_gain"][i][None],
                      "wsT": np.ascontiguousarray(inp["sgu_w_s"][i].transpose(2, 0, 1)), "sbias": inp["sgu_bias"][i].reshape(1, 512),
                      "pw": np.ascontiguousarray(inp["pool_w"][i].transpose(1, 0, 2)), "psc": inp["pool_scale"][i]}
            common.update(ffn_w(layer))
            if final:
                common["fn"] = inp["final_norm"]
            maps = [dict(common, hT=hh[c], invc=_invc_table(c % G == 0)) for c in cores]
            res = launch(("even", final), lambda: build_even(NTOK, final), maps)
            hT = [r["hO"] for r in res]
        else:
            has_vres = i > 0
            vecs = _pack_vecs(inp, i, layer)
            common = {n: inp["rwkv_" + n][i] for n in _RW_NAMES if n not in ("v1", "v2")}
            if has_vres:
                common["v1"] = inp["rwkv_v1"][i - 1]
                common["v2"] = inp["rwkv_v2"][i - 1]
            common.update(vecs=vecs, cst=cst)
            maps = [dict(common, hT=hh[c], **({"vfirst": vfirst[c]} if has_vres else {})) for c in cores]
            res = launch(("pre", has_vres), lambda: build_odd_pre(NTOK, has_vres), maps)
            if not has_vres:
                vfirst = [r["v"] for r in res]
            common2 = {"w_o": inp["rwkv_w_o"][i], "vecs": vecs, "cst": cst}
            common2.update(ffn_w(layer))
            if final:
                common2["fn"] = inp["final_norm"]
            maps2 = []
            for c in cores:
                q = c % G
                st = np.zeros((3, 128, 8, 128), np.float32)
                mv = np.zeros((128, 6), np.float32)
                mv[:, 3:6] = 1.0
                for j in range(min(q, 3)):
                    st[j] = res[c - q + j]["state"]
                    mv[:, j] = 1.0
                    mv[:, 3 + j] = 0.0
                maps2.append(dict(common2, hT=hh[c], y0=res[c]["y0"], z=res[c]["z"], bonus=res[c]["bonus"], g=res[c]["g"], states=st, mvec=mv))
            res2 = launch(("post", final), lambda: build_odd_post(NTOK, final), maps2)
            hT = [r["hO"] for r in res2]
    out = np.zeros((B, G * NTOK, D), np.float32)
    for c in cores:
        out[c // G, (c % G) * NTOK:(c % G + 1) * NTOK] = hT[c].T
    return out


def exchange_halo(P, xin_d, xg_d, sel_d, h_dst, groups):
    G = len(groups[0])
    xgb = P.buf("dram_xg")
    P.coll("AllGather", xin_d, xg_d, groups, writes=[xgb])
    xs, xsb = P.alloc("ex_xs", (128, G, KC, HALO), F32)
    for r in range(G):
        P.dma("sp", xs[:, r, :, :], xg_d[r * D:(r + 1) * D, :].rearrange("(k p) c -> p k c", p=128), reads=[xgb], writes=[xsb])
    sel, selb = P.alloc("ex_sel", (128, 4), F32)
    P.dma("sp", sel[:], sel_d, writes=[selb])
    hal, halb = P.alloc("ex_hal", (128, KC, HALO), F32)
    P.op("dve", lambda e: e.tensor_scalar(out=hal[:], in0=xs[:, 0, :, :], scalar1=sel[:, 0:1], scalar2=None, op0=ALU.mult),
         reads=[xsb, selb], writes=[halb])
    for r in range(1, G):
        P.op("dve", lambda e, r=r: e.scalar_tensor_tensor(out=hal[:], in0=xs[:, r, :, :], scalar=sel[:, r:r + 1], in1=hal[:], op0=ALU.mult, op1=ALU.add),
             reads=[xsb, selb, halb], writes=[halb])
    P.dma("sp", h_dst[:, 0:HALO].rearrange("(k p) c -> p k c", p=128), hal[:], reads=[halb])


def build_fused(NTOK, B, G, depth=4):
    nc = bass.Bass("TRN2", target_bir_lowering=False)
    din, dout, dint = _mk(nc)
    groups = [list(range(b * G, (b + 1) * G)) for b in range(B)]
    hbuf = [din("hT", [D, HALO + NTOK]), dint("hB", [D, HALO + NTOK])]
    hM = dint("hM", [D, NTOK])
    invc = din("invc", [128, 4, HALO]); sel = din("sel", [128, 4]); mvec = din("mvec", [128, 6]); cst = din("cst", [128, CST_W])
    xin = dint("xin", [D, HALO]); xg = dint("xg", [G * D, HALO])
    y0 = dint("y0", [D, NTOK]); z = dint("z", [D, NTOK]); bonus = dint("bonus", [D, NTOK]); g = dint("g", [D, NTOK]); vf = dint("vf", [D, NTOK])
    state = dint("state", [128, 1024]); sg_ = dint("sgath", [max(G, 3) * 128, 1024])
    out = dout("outT", [D, NTOK])
    W = {}
    for layer in range(depth):
        L = "L%d_" % layer
        for n, sh in (("wg", [D, FH]), ("wu", [D, FH]), ("wd", [FH, D]), ("gn", [D])):
            W[L + n] = din(L + n, sh)
        if layer % 2 == 0:
            for n, sh in (("w_in", [D, 1536]), ("w_out", [D, D]), ("mg", [D]), ("sg", [1, 512]), ("wsT", [128, 4, 128]), ("sbias", [1, 512]),
                          ("pw", [128, 4, 128]), ("psc", [512])):
                W[L + n] = din(L + n, sh)
        else:
            for n in _RW_NAMES:
                if n in ("v1", "v2") and layer < 2:
                    continue
                W[L + n] = din(L + n, _RW_SHAPES[n])
            W[L + "w_o"] = din(L + "w_o", [D, D])
            W[L + "vecs"] = din(L + "vecs", [128, NVEC * 8])
    fn = din("fn", [D])
    with ExitStack() as stack:
        P = Prog(nc, stack)
        C = Ctx(P, 512)
        P.persist()
        block = stack.enter_context(nc.Block())
        cur = 0
        stores = []
        if G < 3:
            zt, ztb = P.alloc("zfill", (128, 1024), F32)
            P.op("pool", lambda e: e.memset(zt[:], 0.0), writes=[ztb])
            for r in range(G, 3):
                P.dma("sp", sg_[r * 128:(r + 1) * 128, :], zt[:], reads=[ztb])
            P.phase_reset()
        for layer in range(depth):
            L = "L%d_" % layer
            final = layer == depth - 1
            hin = hbuf[cur]
            hnext = hbuf[1 - cur]
            if layer > 0:
                exchange_halo(P, xin, xg, sel, hin, groups)
                P.phase_reset()
            if layer % 2 == 0:
                phase_ab(P, C, hin, hM, W[L + "w_in"], W[L + "w_out"], W[L + "mg"], W[L + "sg"], W[L + "wsT"], W[L + "sbias"], W[L + "pw"],
                         W[L + "psc"], invc, NTOK // 512)
                P.phase_reset()
            else:
                has_vres = layer >= 3
                Wd = {n: W[L + n] for n in _RW_NAMES if (L + n) in W}
                phase_rwkv_pre(P, C, hin, Wd, W[L + "vecs"], cst, NTOK // RW_T, RW_T, y0, z, bonus, g, vf, vf, state.rearrange("p (c x) -> p c x", c=8), has_vres)
                P.phase_reset()
                sgb = P.buf("dram_sg")
                P.coll("AllGather", state, sg_[0:G * 128, :], groups, writes=[sgb])
                P.phase_reset()
                phase_rwkv_post(P, C, hin, hM, y0, z, bonus, g, sg_[0:3 * 128, :].rearrange("(j p) (c x) -> j p c x", p=128, c=8), mvec,
                                W[L + "w_o"], W[L + "vecs"], cst, NTOK // 256, 256)
                P.phase_reset()
            if final:
                stores = phase_ffn(P, C, hM, None, W[L + "wg"], W[L + "wu"], W[L + "wd"], W[L + "gn"], NTOK // 512, final_gain=fn, outT=out)
            else:
                phase_ffn(P, C, hM, hnext[:, HALO:HALO + NTOK], W[L + "wg"], W[L + "wu"], W[L + "wd"], W[L + "gn"], NTOK // 512, halo_out=xin)
                P.phase_reset()
            cur = 1 - cur
        P.emit(block, final_waits=stores)
    return nc


def run_fused(inp, B, G, NTOK, depth=4):
    NC = B * G
    cores = list(range(NC))
    inp = {k: np.asarray(v, np.float32) for k, v in inp.items()}
    x = inp["x"]
    hT = [np.ascontiguousarray(x[c // G, (c % G) * NTOK:(c % G + 1) * NTOK].T) for c in cores]
    hh = _with_halo(hT, G)
    common = {"cst": make_cst(), "fn": inp["final_norm"]}
    for layer in range(depth):
        L = "L%d_" % layer
        i = layer // 2
        common[L + "wg"] = inp["ffn_w_gate"][layer]
        common[L + "wu"] = inp["ffn_w_up"][layer]
        common[L + "wd"] = inp["ffn_w_down"][layer]
        common[L + "gn"] = inp["ffn_norm"][layer]
        if layer % 2 == 0:
            common[L + "w_in"] = inp["ab_w_in"][i]
            common[L + "w_out"] = inp["ab_w_out"][i]
            common[L + "mg"] = inp["mix_norm"][layer]
            common[L + "sg"] = inp["sgu_gain"][i][None]
            common[L + "wsT"] = np.ascontiguousarray(inp["sgu_w_s"][i].transpose(2, 0, 1))
            common[L + "sbias"] = inp["sgu_bias"][i].reshape(1, 512)
            common[L + "pw"] = np.ascontiguousarray(inp["pool_w"][i].transpose(1, 0, 2))
            common[L + "psc"] = inp["pool_scale"][i]
        else:
            for n in _RW_NAMES:
                if n in ("v1", "v2"):
                    if i > 0:
                        common[L + n] = inp["rwkv_" + n][i - 1]
                else:
                    common[L + n] = inp["rwkv_" + n][i]
            common[L + "w_o"] = inp["rwkv_w_o"][i]
            common[L + "vecs"] = _pack_vecs(inp, i, layer)
    maps = []
    for c in cores:
        q = c % G
        sel = np.zeros((128, 4), np.float32)
        if q > 0:
            sel[:, q - 1] = 1.0
        mv = np.zeros((128, 6), np.float32)
        mv[:, 3:6] = 1.0
        for j in range(min(q, 3)):
            mv[:, j] = 1.0
            mv[:, 3 + j] = 0.0
        maps.append(dict(common, hT=hh[c], invc=_invc_table(q == 0), sel=sel, mvec=mv))
    nc = build_fused(NTOK, B, G, depth)
    res = run_bass_kernel_spmd(nc, maps, core_ids=cores).results
    out = np.zeros((B, G * NTOK, D), np.float32)
    for c in cores:
        out[c // G, (c % G) * NTOK:(c % G + 1) * NTOK] = res[c]["outT"].T
    return out


def kernel(**inputs):
    return run_fused(inputs, 2, 4, 4096)
```

```python
import numpy as np
from contextlib import ExitStack
import concourse.bass as bass
import concourse.mybir as mybir
from concourse.bass_utils import run_bass_kernel_spmd

F32 = mybir.dt.float32
BF16 = mybir.dt.bfloat16
AF = mybir.ActivationFunctionType
ALU = mybir.AluOpType
AX = mybir.AxisListType

D = 1024
KC = 8
FH = 2816
HC = 22
NCORES = 8
EPS = 1e-6


class Buf:
    __slots__ = ("name", "w", "r", "sem_key", "base")

    def __init__(self, name, base=None):
        self.name = name
        self.base = base if base is not None else name
        self.w = None
        self.r = {}
        self.sem_key = None


class Op:
    __slots__ = ("eng", "fn", "deps", "dma", "sig", "need", "idx", "inc")

    def __init__(self, eng, fn, dma):
        self.inc = 16
        self.eng = eng
        self.fn = fn
        self.deps = []
        self.dma = dma
        self.sig = None
        self.need = False


ENGS = ("pe", "act", "dve", "pool", "sp")
CC_INC = 1


class _Rec:
    def __init__(self):
        self.call = None

    def __getattr__(self, name):
        def f(*a, **k):
            self.call = (name, a, k)
            return None
        return f


class Prog:
    SB_LO = 16512
    SB_HI = 229376

    def __init__(self, nc, stack):
        self.nc = nc
        self.stack = stack
        self.ops = {e: [] for e in ENGS}
        self.sems = {}
        self.dma_cnt = {}
        self.nbuf = 0
        self.sb_ptr = self.SB_LO
        self.sb_mark = self.SB_LO
        self.pending_dma = []
        self.nalloc = 0

    def alloc(self, name, shape, dt):
        esz = 4 if dt == F32 else 2
        n = 1
        for d in shape[1:]:
            n *= d
        nbytes = (n * esz + 63) // 64 * 64
        off = self.sb_ptr
        assert off + nbytes <= self.SB_HI, "SBUF overflow: %s needs %d at %d" % (name, nbytes, off)
        self.sb_ptr += nbytes
        self.nalloc += 1
        t = self.nc.alloc_sbuf_tensor_at("%s_%d" % (name, self.nalloc), list(shape), dt, offset=off)
        return t, self.buf(name)

    def persist(self):
        self.sb_mark = self.sb_ptr

    def phase_reset(self):
        lasts = []
        for e in ENGS:
            for o in reversed(self.ops[e]):
                if not o.dma and o.fn is not None:
                    lasts.append(o)
                    break
        deps = lasts + self.pending_dma
        self.pending_dma = []
        for e in ENGS:
            o = Op(e, None, False)
            for d in deps:
                if d.eng == e and not d.dma:
                    continue
                o.deps.append(d)
                d.need = True
            self.ops[e].append(o)
        self.sb_ptr = self.sb_mark

    def sem(self, key):
        if key not in self.sems:
            self.sems[key] = self.stack.enter_context(self.nc.semaphore("s_%s" % (str(key).replace(" ", "_"))))
        return self.sems[key]

    def buf(self, name):
        self.nbuf += 1
        return Buf("%s_%d" % (name, self.nbuf), name)

    def sb(self, name, shape, dt):
        return self.alloc(name, shape, dt)

    def ps(self, name, shape=(128, 512), dt=F32):
        t = self.stack.enter_context(self.nc.psum_tensor(name, list(shape), dt))
        return t, self.buf(name)

    def _add(self, op, reads, writes):
        deps = []
        for b in reads:
            if b.w is not None:
                deps.append(b.w)
        for b in writes:
            if b.w is not None:
                deps.append(b.w)
            deps.extend(b.r.values())
        seen = set()
        for d in deps:
            if d is op or id(d) in seen:
                continue
            seen.add(id(d))
            if (not d.dma) and d.eng == op.eng and op.eng == "pe" and not op.dma:
                continue
            op.deps.append(d)
            d.need = True
        for b in reads:
            b.r[op.sig[0] if op.dma else op.eng] = op
        for b in writes:
            b.w = op
            b.r = {}
        self.ops[op.eng].append(op)
        return op

    def op(self, eng, fn, reads=(), writes=()):
        r = _Rec()
        fn(r)
        assert r.call is not None
        name, a, k = r.call
        return self._add(Op(eng, lambda e, name=name, a=a, k=k: getattr(e, name)(*a, **k), False), reads, writes)

    def dma(self, eng, out_ap, in_ap, reads=(), writes=(), semkey=None, slow=False):
        kb = semkey if semkey is not None else (writes[0] if writes else reads[0])
        key = ("d", kb.base, "w" if writes and kb is writes[0] else "r")
        self.sem(key)
        n = self.dma_cnt.get(key, 0) + 1
        self.dma_cnt[key] = n
        if slow:
            o = Op(eng, lambda e, out_ap=out_ap, in_ap=in_ap: e.dma_start(out=out_ap, in_=in_ap, allow_slow_non_contiguous=True), True)
        else:
            o = Op(eng, lambda e, out_ap=out_ap, in_ap=in_ap: e.dma_start(out=out_ap, in_=in_ap), True)
        o.sig = (key, 16 * n)
        o.need = True
        self.pending_dma.append(o)
        return self._add(o, reads, writes)

    def coll(self, kind, in_ap, out_ap, groups, reads=(), writes=()):
        key = ("cc",)
        self.sem(key)
        n = self.dma_cnt.get(key, 0) + 1
        self.dma_cnt[key] = n
        o = Op("pool", lambda e: e.collective_compute(kind, ALU.bypass, replica_groups=groups, ins=[in_ap.opt()], outs=[out_ap.opt()]), True)
        o.inc = CC_INC
        o.sig = (key, CC_INC * n)
        o.need = True
        self.pending_dma.append(o)
        return self._add(o, reads, writes)

    def emit(self, block, final_waits=()):
        nc = self.nc
        EPOCH = 3000
        for e in ENGS:
            c = 0
            ep = 0
            for o in self.ops[e]:
                if o.dma:
                    continue
                if o.need:
                    c += 1
                    if c > EPOCH:
                        c = 1
                        ep += 1
                    o.sig = (("eng", e, ep), c)
                    self.sem(("eng", e, ep))
        self.maxsig = {e: 0 for e in ENGS}

        def run(engname, eng):
            seen = {}
            for o in self.ops[engname]:
                mx = {}
                for d in o.deps:
                    k, v = d.sig
                    if v > mx.get(k, 0):
                        mx[k] = v
                for k, v in mx.items():
                    if seen.get(k, 0) >= v:
                        continue
                    seen[k] = v
                    eng.wait_ge(self.sems[k], v)
                if o.fn is None:
                    continue
                ins = o.fn(eng)
                if o.need:
                    k, v = o.sig
                    if o.dma:
                        ins.then_inc(self.sems[k], o.inc)
                    else:
                        ins.then_inc(self.sems[k], 1)
            if engname == "sp":
                mx = {}
                for o in final_waits:
                    k, v = o.sig
                    if v > mx.get(k, 0):
                        mx[k] = v
                for k, v in mx.items():
                    if seen.get(k, 0) < v:
                        eng.wait_ge(self.sems[k], v)

        @block.tensor
        def _(eng):
            run("pe", eng)

        @block.scalar
        def _(eng):
            run("act", eng)

        @block.vector
        def _(eng):
            run("dve", eng)

        @block.gpsimd
        def _(eng):
            run("pool", eng)

        @block.sync
        def _(eng):
            run("sp", eng)


class Ctx:
    def __init__(self, P, T):
        self.P = P
        self.T = T
        nc = P.nc
        self.banks = [P.ps("bank%d" % i) for i in range(8)]
        self.bank_i = 0
        self.ones, self.ones_b = P.sb("ones_mean", (128, 128), BF16)
        P.op("pool", lambda e: e.memset(self.ones[:], 1.0 / D), writes=[self.ones_b])
        self.eps_t, self.eps_b = P.sb("eps", (128, 2), F32)
        P.op("pool", lambda e: e.memset(self.eps_t[:], EPS), writes=[self.eps_b])
        self.dq = 0

    def bank(self):
        for _ in range(8):
            t, b = self.banks[self.bank_i % 8]
            self.bank_i += 1
            if b.w is None or b.r:
                return t, b
        raise RuntimeError("all 8 PSUM banks hold unconsumed results")

    def dma_eng(self):
        self.dq += 1
        return "sp"


def load_vec_cols(P, dram_vec, name, n_chunks):
    t, b = P.sb(name, (128, n_chunks), F32)
    src = dram_vec.rearrange("(k p) -> p k", p=128)
    P.dma("sp", t[:], src, writes=[b], slow=True)
    return t, b


def rmsnorm_tile(P, C, h, hb, gain, gainb, hn, hnb, sq, sqb, rstd, rstdb, T):
    for kc in range(KC):
        P.op("pool" if kc % 2 else "dve",
             lambda e, kc=kc: e.tensor_tensor(out=sq[:, kc, :T], in0=h[:, kc, :T], in1=h[:, kc, :T], op=ALU.mult),
             reads=[hb], writes=[sqb[kc]])
    pt, pb = C.bank()
    for kc in range(KC):
        P.op("pe", lambda e, kc=kc: e.matmul(pt[:, :T], C.ones[:], sq[:, kc, :T], start=(kc == 0), stop=(kc == KC - 1)),
             reads=[sqb[kc], C.ones_b], writes=[pb])
    P.op("act", lambda e: e.activation(out=rstd[:, :T], in_=pt[:, :T], func=AF.Sqrt, bias=C.eps_t[:, 0:1]),
         reads=[pb, C.eps_b], writes=[rstdb])
    P.op("dve", lambda e: e.reciprocal(out=rstd[:, :T], in_=rstd[:, :T]), reads=[rstdb], writes=[rstdb])
    for kc in range(KC):
        P.op("dve",
             lambda e, kc=kc: e.scalar_tensor_tensor(out=hn[:, kc, :T], in0=h[:, kc, :T], scalar=gain[:, kc:kc + 1],
                                                     in1=rstd[:, :T], op0=ALU.mult, op1=ALU.mult),
             reads=[hb, rstdb, gainb], writes=[hnb[kc]])


def phase_ffn(P, C, hT_in, hT_out, w_gate, w_up, w_down, gain_vec, NT, final_gain=None, outT=None, halo_out=None):
    nc = P.nc
    T = C.T
    sbl = P.alloc

    wg, wgb = sbl("wg", (128, KC, FH), BF16)
    wu, wub = sbl("wu", (128, KC, FH), BF16)
    wd, wdb = sbl("wd", (128, HC, D), BF16)
    P.dma("pool", wg[:], w_gate.rearrange("(k p) n -> p k n", p=128), writes=[wgb])
    P.dma("pool", wu[:], w_up.rearrange("(k p) n -> p k n", p=128), writes=[wub])
    P.dma("pool", wd[:], w_down.rearrange("(c p) n -> p c n", p=128), writes=[wdb])
    gain, gainb = sbl("ffn_gain", (128, KC), F32)
    P.dma("sp", gain[:], gain_vec.rearrange("(k p) -> p k", p=128), writes=[gainb], slow=True)
    if final_gain is not None:
        fg, fgb = sbl("fin_gain", (128, KC), F32)
        P.dma("sp", fg[:], final_gain.rearrange("(k p) -> p k", p=128), writes=[fgb], slow=True)

    hs = [sbl("ffn_h%d" % i, (128, KC, T), F32) for i in range(2)]
    hn, _ = sbl("ffn_hn", (128, KC, T), BF16)
    hnb = [P.buf("hn%d" % k) for k in range(KC)]
    rstd, rstdb = sbl("ffn_rstd", (128, T), F32)
    act, _ = sbl("ffn_act", (128, HC, T), BF16)
    actb = [P.buf("act%d" % k) for k in range(HC)]
    sq, sqb = act, actb
    sg = [sbl("ffn_sg%d" % i, (128, T), F32) for i in range(2)]

    hin = hT_in.rearrange("(k p) t -> p k t", p=128)
    hout = hT_out.rearrange("(k p) t -> p k t", p=128) if hT_out is not None else None
    stores = []
    for it in range(NT):
        h, hb = hs[it % 2]
        t0 = it * T
        P.dma("sp", h[:], hin[:, :, t0:t0 + T], writes=[hb])
        rmsnorm_tile(P, C, h, hb, gain, gainb, hn, hnb, sq, sqb, rstd, rstdb, T)
        for hc in range(HC):
            pg, pgb = C.bank()
            pu, pub = C.bank()
            for kc in range(KC):
                P.op("pe", lambda e, kc=kc, hc=hc, pg=pg: e.matmul(pg[:, :T], wg[:, kc, hc * 128:(hc + 1) * 128], hn[:, kc, :T],
                                                                    start=(kc == 0), stop=(kc == KC - 1)),
                     reads=[wgb, hnb[kc]], writes=[pgb])
            for kc in range(KC):
                P.op("pe", lambda e, kc=kc, hc=hc, pu=pu: e.matmul(pu[:, :T], wu[:, kc, hc * 128:(hc + 1) * 128], hn[:, kc, :T],
                                                                    start=(kc == 0), stop=(kc == KC - 1)),
                     reads=[wub, hnb[kc]], writes=[pub])
            s, sb_ = sg[hc % 2]
            P.op("act", lambda e, s=s, pg=pg: e.activation(out=s[:, :T], in_=pg[:, :T], func=AF.Silu),
                 reads=[pgb], writes=[sb_])
            P.op("dve", lambda e, s=s, pu=pu, hc=hc: e.tensor_tensor(out=act[:, hc, :T], in0=pu[:, :T], in1=s[:, :T], op=ALU.mult),
                 reads=[pub, sb_], writes=[actb[hc]])
        for dc in range(KC):
            po, pob = C.bank()
            for hc in range(HC):
                P.op("pe", lambda e, hc=hc, dc=dc, po=po: e.matmul(po[:, :T], wd[:, hc, dc * 128:(dc + 1) * 128], act[:, hc, :T],
                                                                    start=(hc == 0), stop=(hc == HC - 1)),
                     reads=[wdb, actb[hc]], writes=[pob])
            P.op("dve", lambda e, dc=dc, po=po, h=h: e.tensor_tensor(out=h[:, dc, :T], in0=po[:, :T], in1=h[:, dc, :T], op=ALU.add),
                 reads=[pob, hb], writes=[hb])
        if final_gain is None:
            stores.append(P.dma("sp", hout[:, :, t0:t0 + T], h[:], reads=[hb], writes=[]))
            if halo_out is not None and it == NT - 1:
                stores.append(P.dma("sp", halo_out.rearrange("(k p) c -> p k c", p=128), h[:, :, T - HALO:T], reads=[hb], writes=[]))
        else:
            hnf, _ = hn, None
            for kc in range(KC):
                P.op("pool" if kc % 2 else "dve",
                     lambda e, kc=kc, h=h: e.tensor_tensor(out=sq[:, kc, :T], in0=h[:, kc, :T], in1=h[:, kc, :T], op=ALU.mult),
                     reads=[hb], writes=[sqb[kc]])
            pt, pb = C.bank()
            for kc in range(KC):
                P.op("pe", lambda e, kc=kc, pt=pt: e.matmul(pt[:, :T], C.ones[:], sq[:, kc, :T], start=(kc == 0), stop=(kc == KC - 1)),
                     reads=[sqb[kc], C.ones_b], writes=[pb])
            P.op("act", lambda e, pt=pt: e.activation(out=rstd[:, :T], in_=pt[:, :T], func=AF.Sqrt, bias=C.eps_t[:, 0:1]),
                 reads=[pb, C.eps_b], writes=[rstdb])
            P.op("dve", lambda e: e.reciprocal(out=rstd[:, :T], in_=rstd[:, :T]), reads=[rstdb], writes=[rstdb])
            for kc in range(KC):
                P.op("dve",
                     lambda e, kc=kc, h=h: e.scalar_tensor_tensor(out=h[:, kc, :T], in0=h[:, kc, :T], scalar=fg[:, kc:kc + 1],
                                                                  in1=rstd[:, :T], op0=ALU.mult, op1=ALU.mult),
                     reads=[hb, rstdb, fgb], writes=[hb])
            oo = outT.rearrange("(k p) t -> p k t", p=128)
            stores.append(P.dma("sp", oo[:, :, t0:t0 + T], h[:], reads=[hb], writes=[]))
    return stores


HALO = 16
AW = 512


def phase_ab(P, C, hT_in, hT_out, w_in, w_out, mixgain_vec, sgu_gain_row, wsT_d, sgu_bias_row, pool_w_d, pool_scale_vec,
             invc_d, NT):
    nc = P.nc
    T = C.T
    TH = T + HALO
    A = P.alloc
    win, winb = A("ab_win", (128, KC, 3 * AW), BF16)
    wout, woutb = A("ab_wout", (128, KC, D), BF16)
    P.dma("pool", win[:], w_in.rearrange("(k p) n -> p k n", p=128), writes=[winb])
    P.dma("pool", wout[:], w_out.rearrange("(k p) n -> p k n", p=128), writes=[woutb])
    wsT, wsTb = A("ab_wsT", (128, 4, 128), BF16)
    P.dma("pool", wsT[:], wsT_d, writes=[wsTb])
    P.op("pool", lambda e: e.memset(wsT[64:128, :, 0:64], 0.0), reads=[wsTb], writes=[wsTb])
    pw, pwb = A("ab_pw", (128, 4, 128), BF16)
    P.dma("pool", pw[:], pool_w_d, writes=[pwb])
    gain, gainb = A("ab_gain", (128, KC), F32)
    P.dma("sp", gain[:], mixgain_vec.rearrange("(k p) -> p k", p=128), writes=[gainb], slow=True)
    psc, pscb = A("ab_psc", (128, 4), F32)
    P.dma("sp", psc[:], pool_scale_vec.rearrange("(k p) -> p k", p=128), writes=[pscb], slow=True)
    invc, invcb = A("ab_invc", (128, 4, HALO), F32)
    P.dma("sp", invc[:], invc_d, writes=[invcb])
    rows, rowsb = A("ab_rows", (1, 2 * AW), F32)
    P.dma("sp", rows[0:1, 0:AW], sgu_gain_row, writes=[rowsb])
    P.dma("sp", rows[0:1, AW:2 * AW], sgu_bias_row, writes=[rowsb])
    onesf, onesfb = A("ab_onesf", (1, 128), F32)
    P.op("pool", lambda e: e.memset(onesf[:], 1.0), writes=[onesfb])
    gbc, gbcb = A("ab_gbc", (128, AW), F32)
    pt, pb = C.bank()
    P.op("pe", lambda e: e.matmul(pt[:, :AW], onesf[0:1, :], rows[0:1, 0:AW], start=True, stop=True),
         reads=[onesfb, rowsb], writes=[pb])
    P.op("dve", lambda e: e.tensor_copy(out=gbc[:], in_=pt[:, :AW]), reads=[pb], writes=[gbcb])

    hs = [A("ab_h%d" % i, (128, KC, TH), F32) for i in range(2)]
    hn, _ = A("ab_hn", (128, KC, TH), BF16)
    hnb = [P.buf("abhn%d" % k) for k in range(KC)]
    sq, _ = A("ab_sq", (128, KC, TH), BF16)
    sqb = [P.buf("absq%d" % k) for k in range(KC)]
    rstd, rstdb = A("ab_rstd", (128, TH), F32)
    u, _ = A("ab_u", (128, 4, T), BF16)
    ub = [P.buf("abu%d" % k) for k in range(4)]
    vs = [A("ab_v%d" % i, (128, AW), F32) for i in range(4)]
    vq = [A("ab_vq%d" % i, (128, AW), F32) for i in range(4)]
    vn = [A("ab_vn%d" % i, (128, AW), BF16) for i in range(4)]
    ss = [A("ab_ss%d" % i, (128, 2), F32) for i in range(4)]
    zp, _ = A("ab_zp", (128, 4, TH), F32)
    zpb = [P.buf("abzp%d" % k) for k in range(4)]
    sa_all = [[A("ab_sa%d_%d" % (g_, i), (128, TH), F32) for i in range(2)] for g_ in range(4)]
    pooled, _ = A("ab_pooled", (128, 4, T), BF16)
    pooledb = [P.buf("abpl%d" % k) for k in range(4)]
    ptmps = [A("ab_ptmp%d" % g_, (128, HALO), F32) for g_ in range(4)]
    cat, _ = A("ab_cat", (128, KC, T), BF16)
    catb = [P.buf("abcat%d" % k) for k in range(KC)]

    hin = hT_in.rearrange("(k p) t -> p k t", p=128)
    hout = hT_out.rearrange("(k p) t -> p k t", p=128)
    stores = []
    for it in range(NT):
        h, hb = hs[it % 2]
        t0 = it * T
        P.dma("sp", h[:], hin[:, :, t0:t0 + TH], writes=[hb])
        for kc in range(KC):
            P.op("pool" if kc % 2 else "dve",
                 lambda e, kc=kc, h=h: e.tensor_tensor(out=sq[:, kc, :], in0=h[:, kc, :], in1=h[:, kc, :], op=ALU.mult),
                 reads=[hb], writes=[sqb[kc]])
        for (c0, c1) in ((0, HALO), (HALO, TH)):
            pt, pb = C.bank()
            n = c1 - c0
            for kc in range(KC):
                P.op("pe", lambda e, kc=kc, pt=pt, c0=c0, c1=c1, n=n: e.matmul(pt[:, :n], C.ones[:], sq[:, kc, c0:c1],
                                                                                 start=(kc == 0), stop=(kc == KC - 1)),
                     reads=[sqb[kc], C.ones_b], writes=[pb])
            P.op("act", lambda e, pt=pt, c0=c0, c1=c1, n=n: e.activation(out=rstd[:, c0:c1], in_=pt[:, :n], func=AF.Sqrt,
                                                                          bias=C.eps_t[:, 0:1]),
                 reads=[pb, C.eps_b], writes=[rstdb])
        P.op("dve", lambda e: e.reciprocal(out=rstd[:], in_=rstd[:]), reads=[rstdb], writes=[rstdb])
        for kc in range(KC):
            P.op("dve", lambda e, kc=kc, h=h: e.scalar_tensor_tensor(out=hn[:, kc, :], in0=h[:, kc, :], scalar=gain[:, kc:kc + 1],
                                                                     in1=rstd[:], op0=ALU.mult, op1=ALU.mult),
                 reads=[hb, rstdb, gainb], writes=[hnb[kc]])
        for c in range(4):
            pt, pb = C.bank()
            for kc in range(KC):
                P.op("pe", lambda e, kc=kc, c=c, pt=pt: e.matmul(pt[:, :T], win[:, kc, c * 128:(c + 1) * 128], hn[:, kc, HALO:TH],
                                                                  start=(kc == 0), stop=(kc == KC - 1)),
                     reads=[winb, hnb[kc]], writes=[pb])
            P.op("act", lambda e, c=c, pt=pt: e.activation(out=u[:, c, :], in_=pt[:, :T], func=AF.Gelu),
                 reads=[pb], writes=[ub[c]])
        def pool_chain(g):
            sa = sa_all[g]
            ptmp, ptmpb = ptmps[g]
            pt, pb = C.bank()
            pt2, pb2 = C.bank()
            for kc in range(KC):
                P.op("pe", lambda e, kc=kc, g=g, pt=pt: e.matmul(pt[:, :T], win[:, kc, 2 * AW + g * 128:2 * AW + (g + 1) * 128],
                                                                  hn[:, kc, HALO:TH], start=(kc == 0), stop=(kc == KC - 1)),
                     reads=[winb, hnb[kc]], writes=[pb])
            for kc in range(KC):
                P.op("pe", lambda e, kc=kc, g=g, pt2=pt2: e.matmul(pt2[:, :HALO], win[:, kc, 2 * AW + g * 128:2 * AW + (g + 1) * 128],
                                                                    hn[:, kc, 0:HALO], start=(kc == 0), stop=(kc == KC - 1)),
                     reads=[winb, hnb[kc]], writes=[pb2])
            P.op("act", lambda e, g=g, pt=pt: e.copy(out=zp[:, g, HALO:TH], in_=pt[:, :T]), reads=[pb], writes=[zpb[g]])
            P.op("dve", lambda e, g=g, pt2=pt2: e.tensor_copy(out=zp[:, g, 0:HALO], in_=pt2[:, :HALO]), reads=[pb2, zpb[g]], writes=[zpb[g]])
            src, srcb = zp[:, g, :], zpb[g]
            lo = 0
            for lv in range(g + 1):
                step = 1 << lv
                dst, dstb = sa[lv % 2]
                lo2 = lo + step
                P.op("pool", lambda e, src=src, dst=dst, lo2=lo2, step=step: e.tensor_tensor(
                    out=dst[:, lo2:TH], in0=src[:, lo2:TH], in1=src[:, lo2 - step:TH - step], op=ALU.add),
                    reads=[srcb], writes=[dstb])
                src, srcb, lo = dst, dstb, lo2
                yield
            wsz = float(1 << (g + 1))
            P.op("dve", lambda e, src=src, g=g, wsz=wsz: e.scalar_tensor_tensor(
                out=pooled[:, g, :], in0=src[:, HALO:TH], scalar=1.0 / wsz, in1=zp[:, g, HALO:TH], op0=ALU.mult, op1=ALU.subtract),
                reads=[srcb, zpb[g]], writes=[pooledb[g]])
            if it == 0:
                P.op("dve", lambda e, src=src, g=g: e.tensor_tensor(out=ptmp[:], in0=src[:, HALO:2 * HALO], in1=invc[:, g, :], op=ALU.mult),
                     reads=[srcb, invcb], writes=[ptmpb])
                P.op("dve", lambda e, g=g: e.tensor_tensor(out=pooled[:, g, 0:HALO], in0=ptmp[:], in1=zp[:, g, HALO:2 * HALO], op=ALU.subtract),
                     reads=[ptmpb, zpb[g], pooledb[g]], writes=[pooledb[g]])
            yield
            pt, pb = C.bank()
            P.op("pe", lambda e, g=g, pt=pt: e.matmul(pt[:, :T], pw[:, g, :], pooled[:, g, :], start=True, stop=True),
                 reads=[pwb, pooledb[g]], writes=[pb])
            yield
            P.op("dve", lambda e, g=g, pt=pt: e.tensor_scalar(out=cat[:, 4 + g, :], in0=pt[:, :T], scalar1=psc[:, g:g + 1], scalar2=None,
                                                              op0=ALU.mult),
                 reads=[pb, pscb], writes=[catb[4 + g]])
        def gmlp_chain(n):
            v, vb = vs[n % 4]
            q_, qb = vq[n % 4]
            vnn, vnb = vn[n % 4]
            s_, sb_ = ss[n % 4]
            c0 = HALO + n * 128
            pt, pb = C.bank()
            for kc in range(KC):
                P.op("pe", lambda e, kc=kc, pt=pt, c0=c0: e.matmul(pt[:, :AW], hn[:, kc, c0:c0 + 128], win[:, kc, AW:2 * AW],
                                                                    start=(kc == 0), stop=(kc == KC - 1)),
                     reads=[winb, hnb[kc]], writes=[pb])
            yield
            P.op("act", lambda e, v=v, pt=pt: e.activation(out=v[:], in_=pt[:, :AW], func=AF.Gelu), reads=[pb], writes=[vb])
            yield
            P.op("pool", lambda e, v=v, q_=q_: e.tensor_tensor(out=q_[:], in0=v[:], in1=v[:], op=ALU.mult), reads=[vb], writes=[qb])
            yield
            P.op("dve", lambda e, q_=q_, s_=s_: e.reduce_sum(out=s_[:, 0:1], in_=q_[:], axis=AX.X), reads=[qb], writes=[sb_])
            P.op("act", lambda e, s_=s_: e.activation(out=s_[:, 1:2], in_=s_[:, 0:1], func=AF.Sqrt, bias=C.eps_t[:, 0:1], scale=1.0 / AW),
                 reads=[sb_, C.eps_b], writes=[sb_])
            P.op("dve", lambda e, s_=s_: e.reciprocal(out=s_[:, 1:2], in_=s_[:, 1:2]), reads=[sb_], writes=[sb_])
            yield
            P.op("dve", lambda e, v=v, vnn=vnn, s_=s_: e.scalar_tensor_tensor(out=vnn[:], in0=v[:], scalar=s_[:, 1:2], in1=gbc[:],
                                                                              op0=ALU.mult, op1=ALU.mult),
                 reads=[vb, sb_, gbcb], writes=[vnb])
            yield
            pm, pmb = C.bank()
            for hd in range(4):
                P.op("pe", lambda e, hd=hd, pm=pm, vnn=vnn: e.matmul(pm[:, hd * 128:(hd + 1) * 128], vnn[:, hd * 128:(hd + 1) * 128],
                                                                      wsT[:, hd, :], start=True, stop=False),
                     reads=[vnb, wsTb], writes=[pmb])
                P.op("pe", lambda e, hd=hd, pm=pm: e.matmul(pm[:, hd * 128:(hd + 1) * 128], onesf[0:1, :],
                                                             rows[0:1, AW + hd * 128:AW + (hd + 1) * 128], start=False, stop=True),
                     reads=[onesfb, rowsb], writes=[pmb])
            yield
            for hd in range(4):
                P.op("dve", lambda e, hd=hd, pm=pm, n=n: e.tensor_tensor(out=cat[:, hd, n * 128:(n + 1) * 128],
                                                                         in0=pm[:, hd * 128:(hd + 1) * 128],
                                                                         in1=u[:, hd, n * 128:(n + 1) * 128], op=ALU.mult),
                     reads=[pmb, ub[hd]], writes=[catb[hd]])
        gens = [pool_chain(g_) for g_ in range(4)] + [gmlp_chain(n_) for n_ in range(T // 128)]
        while gens:
            for gn_ in list(gens):
                try:
                    next(gn_)
                except StopIteration:
                    gens.remove(gn_)
        for dc in range(KC):
            po, pob = C.bank()
            for c in range(KC):
                P.op("pe", lambda e, c=c, dc=dc, po=po: e.matmul(po[:, :T], wout[:, c, dc * 128:(dc + 1) * 128], cat[:, c, :],
                                                                  start=(c == 0), stop=(c == KC - 1)),
                     reads=[woutb, catb[c]], writes=[pob])
            P.op("dve", lambda e, dc=dc, po=po, h=h: e.tensor_tensor(out=h[:, dc, HALO:TH], in0=po[:, :T], in1=h[:, dc, HALO:TH], op=ALU.add),
                 reads=[pob, hb], writes=[hb])
        stores.append(P.dma("sp", hout[:, :, t0:t0 + T], h[:, :, HALO:TH], reads=[hb], writes=[]))
    return stores


KAPPA = float(np.exp(-0.5))
RW_T = 128
NVEC = 15
V_GAIN, V_MU, V_W0, V_A0, V_KK, V_KA, V_RK, V_V0, V_LNW, V_LNB = 0, 1, 7, 8, 9, 10, 11, 12, 13, 14
CST_W = 1792 + 512


def make_cst():
    c = np.zeros((128, CST_W), np.float32)
    j = np.arange(128)[:, None]
    t = np.arange(128)[None, :]
    strict = (j < t).astype(np.float32)
    incl = (j <= t).astype(np.float32)
    c[:, 0:128] = strict
    c[:, 128:256] = incl
    c[:, 256:384] = strict
    c[:, 384:512] = incl
    low = (t < j).astype(np.float32)
    for u in range(4):
        c[:, 512 + u * 128:512 + (u + 1) * 128] = low
    c[:, 1024:1152] = np.eye(128, dtype=np.float32)
    c[0:64, 1152:1216] = 1.0
    c[64:128, 1216:1280] = 1.0
    seg = np.ones(512, np.float32)
    seg[0::128] = 0.0
    c[:, 1280:1792] = seg[None, :]
    for u in range(4):
        c[:, 1792 + u * 128:1792 + (u + 1) * 128] = np.eye(128, dtype=np.float32)
    return c


def phase_rwkv_pre(P, C, hT_in, Wd, vecs_d, cst_d, NT, T, y0_d, z_d, bonus_d, g_d, v_d, vfirst_d, state_d, has_vres, stage=9, KPAR=4):
    nc = P.nc
    A = P.alloc
    CH = T // 128
    T1 = T + 1
    NU = 2 * CH
    assert NU <= 4

    def wload(name, src, shape):
        t, b = A(name, shape, BF16)
        P.dma("pool", t[:], src, writes=[b])
        return t, b

    wr, wrb = wload("rw_wr", Wd["w_r"].rearrange("(k p) n -> p k n", p=128), (128, KC, D))
    wk, wkb = wload("rw_wk", Wd["w_k"].rearrange("(k p) n -> p k n", p=128), (128, KC, D))
    wv, wvb = wload("rw_wv", Wd["w_v"].rearrange("(k p) n -> p k n", p=128), (128, KC, D))
    w1, w1b = wload("rw_w1", Wd["w1"].rearrange("(k p) n -> p k n", p=128), (128, KC, 64))
    a1, a1b = wload("rw_a1", Wd["a1"].rearrange("(k p) n -> p k n", p=128), (128, KC, 64))
    g1, g1b = wload("rw_g1", Wd["g1"].rearrange("(k p) n -> p k n", p=128), (128, KC, 160))
    w2, w2b = wload("rw_w2", Wd["w2"], (64, D))
    a2, a2b = wload("rw_a2", Wd["a2"], (64, D))
    g2a, g2ab = wload("rw_g2a", Wd["g2"][0:128, :], (128, D))
    g2b, g2bb = wload("rw_g2b", Wd["g2"][128:160, :], (32, D))
    if has_vres:
        v1, v1b = wload("rw_v1", Wd["v1"].rearrange("(k p) n -> p k n", p=128), (128, KC, 32))
        v2, v2b = wload("rw_v2", Wd["v2"], (32, D))
    vecs, vecsb = A("rw_vecs", (128, NVEC * 8), F32)
    P.dma("sp", vecs[:], vecs_d, writes=[vecsb])
    cst, cstb = A("rw_cst", (128, CST_W), F32)
    P.dma("sp", cst[:], cst_d, writes=[cstb])

    def V(i, kc):
        return vecs[:, i * 8 + kc:i * 8 + kc + 1]

    mask4 = cst[:, 0:512]
    maskL = cst[:, 512:512 + NU * 128]
    identf = cst[:, 1024:1152]
    segm = cst[:, 1280:1280 + T]
    identb, identbb = A("rw_identb", (128, 128), BF16)
    P.op("pool", lambda e: e.tensor_copy(out=identb[:], in_=identf), reads=[cstb], writes=[identbb])
    bones, bonesb = A("rw_bones", (128, 128), BF16)
    P.op("pool", lambda e: e.tensor_copy(out=bones[:], in_=cst[:, 1152:1280]), reads=[cstb], writes=[bonesb])
    ident4 = cst[:, 1792:1792 + NU * 128]
    omka, omkab = A("rw_omka", (128, 8), F32)
    P.op("pool", lambda e: e.tensor_scalar(out=omka[:], in0=vecs[:, V_KA * 8:V_KA * 8 + 8], scalar1=-1.0, scalar2=1.0,
                                            op0=ALU.mult, op1=ALU.add), reads=[vecsb], writes=[omkab])
    S_f, _ = A("rw_Sf", (128, KC, 128), F32)
    S_b, _ = A("rw_Sb", (128, KC, 128), BF16)
    Sfb = [P.buf("Sf%d" % c) for c in range(KC)]
    Sbb = [P.buf("Sb%d" % c) for c in range(KC)]
    for c in range(KC):
        P.op("pool", lambda e, c=c: e.memset(S_f[:, c, :], 0.0), writes=[Sfb[c]])
        P.op("pool", lambda e, c=c: e.tensor_copy(out=S_f[0:64, c, 64:128], in_=identf[0:64, 0:64]), reads=[cstb, Sfb[c]], writes=[Sfb[c]])
        P.op("pool", lambda e, c=c: e.tensor_copy(out=S_f[64:128, c, 64:128], in_=identf[64:128, 64:128]), reads=[cstb, Sfb[c]], writes=[Sfb[c]])
        P.op("pool", lambda e, c=c: e.tensor_copy(out=S_b[:, c, :], in_=S_f[:, c, :]), reads=[Sfb[c]], writes=[Sbb[c]])

    h, hb = A("rw_h", (128, KC, T1), F32)
    sq, _ = A("rw_sq", (128, KC, T1), BF16)
    sqb = [P.buf("rwsq%d" % k) for k in range(KC)]
    rstd, rstdb = A("rw_rstd", (128, T1), F32)
    xx, _ = A("rw_xx", (128, KC, T), F32)
    xxb = [P.buf("rwxx%d" % k) for k in range(KC)]
    xi = []
    for i in range(6):
        t_, _ = A("rw_xi%d" % i, (128, KC, T), BF16)
        xi.append((t_, [P.buf("rwxi%d_%d" % (i, k)) for k in range(KC)]))
    t1b_, t1bb = A("rw_t1", (64, T), BF16)
    t2b_, t2bb = A("rw_t2", (64, T), BF16)
    t3a, t3ab = A("rw_t3a", (128, T), BF16)
    t3b, t3bb = A("rw_t3b", (32, T), BF16)
    if has_vres:
        t4b_, t4bb = A("rw_t4", (32, T), BF16)

    PREP_F32 = ("r", "k", "sig", "cs", "a", "e2", "e3", "e4", "kkr", "kkn", "tmp", "kf", "bsc", "g", "bon", "v", "bh", "kh") + (("vf",) if has_vres else ())

    def prep_set(s):
        d = {}
        for nm in PREP_F32:
            if nm in ("kh", "bh", "bsc", "bon"):
                continue
            d[nm] = A("rw_%s%d" % (nm, s), (128, T), F32)
        d["kh"] = d["tmp"]
        d["bh"] = d["sig"]
        d["bsc"] = d["kkr"]
        d["bon"] = d["g"]
        d["nb"] = A("rw_nb%d" % s, (128, CH), F32)
        for nm in ("sqk", "rkk"):
            d[nm] = A("rw_%s%d" % (nm, s), (128, T), BF16)
        return d

    def scan_set(s):
        d = {}
        d["e1"] = A("rw_e1%d" % s, (128, T), F32)
        for nm in ("bt", "kt"):
            d[nm] = A("rw_%s%d" % (nm, s), (128, T), BF16)
        d["ar"] = [A("rw_ar%d_%d" % (s, i), (128, CH, 2, 128), BF16) for i in range(2)]
        for i in range(2):
            P.op("pool", lambda e, t_=d["ar"][i][0]: e.memset(t_[:], 0.0), writes=[d["ar"][i][1]])
        d["bhT"] = A("rw_bhT%d" % s, (128, CH, 128), BF16)
        d["khT"] = A("rw_khT%d" % s, (128, CH, 128), BF16)
        d["vpad"] = A("rw_vpad%d" % s, (128, CH, 2, 128), BF16)
        d["Aall"] = A("rw_Aall%d" % s, (128, NU, 512), BF16)
        d["NN"] = [A("rw_NN%d_%d" % (s, i), (128, NU, 128), BF16) for i in range(2)]
        d["LL"] = [A("rw_LL%d_%d" % (s, i), (128, NU, 128), BF16) for i in range(2)]
        d["Tt"] = A("rw_Tt%d" % s, (128, NU, 128), BF16)
        d["Xs"] = A("rw_Xs%d" % s, (128, 256), BF16)
        d["Us"] = A("rw_Us%d" % s, (128, 256), BF16)
        d["yt"] = A("rw_yt%d" % s, (128, 2, T), F32)
        vp = d["vpad"]
        P.op("pool", lambda e, vp=vp: e.memset(vp[0][:], 0.0), writes=[vp[1]])
        return d

    NPREP = 3
    psets = [prep_set(i) for i in range(NPREP)]
    ssets = [scan_set(i) for i in range(KC)]

    hin = hT_in.rearrange("(k p) t -> p k t", p=128)
    stores = []
    fns = {}

    def prologue(it):
        t0 = it * T
        P.dma("sp", h[:], hin[:, :, HALO - 1 + t0:HALO + t0 + T], writes=[hb])
        yield
        for kc in range(KC):
            P.op("pool", lambda e, kc=kc: e.tensor_tensor(out=sq[:, kc, :], in0=h[:, kc, :], in1=h[:, kc, :], op=ALU.mult),
                 reads=[hb], writes=[sqb[kc]])
        for (c0, c1) in ((0, 1), (1, T1)):
            pt, pb = C.bank()
            n = c1 - c0
            for kc in range(KC):
                P.op("pe", lambda e, kc=kc, pt=pt, c0=c0, c1=c1, n=n: e.matmul(pt[:, :n], C.ones[:], sq[:, kc, c0:c1],
                                                                                 start=(kc == 0), stop=(kc == KC - 1)),
                     reads=[sqb[kc], C.ones_b], writes=[pb])
            P.op("act", lambda e, pt=pt, c0=c0, c1=c1, n=n: e.activation(out=rstd[:, c0:c1], in_=pt[:, :n], func=AF.Sqrt,
                                                                          bias=C.eps_t[:, 0:1]),
                 reads=[pb, C.eps_b], writes=[rstdb])
        P.op("dve", lambda e: e.reciprocal(out=rstd[:], in_=rstd[:]), reads=[rstdb], writes=[rstdb])
        for kc in range(KC):
            P.op("dve", lambda e, kc=kc: e.scalar_tensor_tensor(out=h[:, kc, :], in0=h[:, kc, :], scalar=V(V_GAIN, kc),
                                                                in1=rstd[:], op0=ALU.mult, op1=ALU.mult),
                 reads=[hb, rstdb, vecsb], writes=[hb])
        yield
        for kc in range(KC):
            P.op("pool", lambda e, kc=kc: e.tensor_tensor(out=xx[:, kc, :], in0=h[:, kc, 0:T], in1=h[:, kc, 1:T1], op=ALU.subtract),
                 reads=[hb], writes=[xxb[kc]])
        yield
        for i in range(6):
            yield
            for kc in range(KC):
                P.op("dve", lambda e, kc=kc, i=i: e.scalar_tensor_tensor(out=xi[i][0][:, kc, :], in0=xx[:, kc, :], scalar=V(V_MU + i, kc),
                                                                         in1=h[:, kc, 1:T1], op0=ALU.mult, op1=ALU.add),
                     reads=[xxb[kc], hb, vecsb], writes=[xi[i][1][kc]])
        xr, xw, xk, xv, xa, xg = xi
        yield
        pt, pb = C.bank()
        for kc in range(KC):
            P.op("pe", lambda e, kc=kc, pt=pt: e.matmul(pt[0:64, :T], w1[:, kc, :], xw[0][:, kc, :], start=(kc == 0), stop=(kc == KC - 1)),
                 reads=[w1b, xw[1][kc]], writes=[pb])
        P.op("act", lambda e, pt=pt: e.activation(out=t1b_[:], in_=pt[0:64, :T], func=AF.Tanh), reads=[pb], writes=[t1bb])
        pt, pb = C.bank()
        for kc in range(KC):
            P.op("pe", lambda e, kc=kc, pt=pt: e.matmul(pt[0:64, :T], a1[:, kc, :], xa[0][:, kc, :], start=(kc == 0), stop=(kc == KC - 1)),
                 reads=[a1b, xa[1][kc]], writes=[pb])
        P.op("act", lambda e, pt=pt: e.copy(out=t2b_[:], in_=pt[0:64, :T]), reads=[pb], writes=[t2bb])
        pt, pb = C.bank()
        for kc in range(KC):
            P.op("pe", lambda e, kc=kc, pt=pt: e.matmul(pt[:, :T], g1[:, kc, 0:128], xg[0][:, kc, :], start=(kc == 0), stop=(kc == KC - 1)),
                 reads=[g1b, xg[1][kc]], writes=[pb])
        P.op("act", lambda e, pt=pt: e.activation(out=t3a[:], in_=pt[:, :T], func=AF.Sigmoid), reads=[pb], writes=[t3ab])
        pt, pb = C.bank()
        for kc in range(KC):
            P.op("pe", lambda e, kc=kc, pt=pt: e.matmul(pt[0:32, :T], g1[:, kc, 128:160], xg[0][:, kc, :], start=(kc == 0), stop=(kc == KC - 1)),
                 reads=[g1b, xg[1][kc]], writes=[pb])
        P.op("act", lambda e, pt=pt: e.activation(out=t3b[:], in_=pt[0:32, :T], func=AF.Sigmoid), reads=[pb], writes=[t3bb])
        if has_vres:
            pt, pb = C.bank()
            for kc in range(KC):
                P.op("pe", lambda e, kc=kc, pt=pt: e.matmul(pt[0:32, :T], v1[:, kc, :], xv[0][:, kc, :], start=(kc == 0), stop=(kc == KC - 1)),
                     reads=[v1b, xv[1][kc]], writes=[pb])
            P.op("act", lambda e, pt=pt: e.copy(out=t4b_[:], in_=pt[0:32, :T]), reads=[pb], writes=[t4bb])

        def prep_body(c, Wp, Ws):
            W = dict(Wp)
            W.update(Ws)
            cs_ = slice(c * 128, (c + 1) * 128)

            def proj(wt, wtb, x, dst, dstb, eng="act"):
                pt, pb = C.bank()
                for kc in range(KC):
                    P.op("pe", lambda e, kc=kc, pt=pt: e.matmul(pt[:, :T], wt[:, kc, cs_], x[0][:, kc, :], start=(kc == 0), stop=(kc == KC - 1)),
                         reads=[wtb, x[1][kc]], writes=[pb])
                if eng == "act":
                    P.op("act", lambda e, pt=pt: e.copy(out=dst, in_=pt[:, :T]), reads=[pb], writes=[dstb])
                else:
                    P.op("dve", lambda e, pt=pt: e.tensor_copy(out=dst, in_=pt[:, :T]), reads=[pb], writes=[dstb])

            r_, rb_ = W["r"]
            k_, kb_ = W["k"]
            proj(wr, wrb, xr, r_[:], rb_, "act")
            proj(wk, wkb, xk, k_[:], kb_, "dve")
            vfull, vb_ = W["v"]
            v_ = vfull[:]
            proj(wv, wvb, xv, v_, vb_, "act")
            if has_vres:
                vft, vftb = W["vf"]
                P.dma("sp", vft[:], vfirst_d[c * 128:(c + 1) * 128, t0:t0 + T], writes=[vftb])
            sig, sigb = W["sig"]
            pt, pb = C.bank()
            P.op("pe", lambda e, pt=pt: e.matmul(pt[:, :T], w2[0:64, cs_], t1b_[0:64, :], start=True, stop=True), reads=[w2b, t1bb], writes=[pb])
            P.op("act", lambda e, pt=pt: e.activation(out=sig[:], in_=pt[:, :T], func=AF.Sigmoid, bias=V(V_W0, c)),
                 reads=[pb, vecsb], writes=[sigb])
            a_, ab_ = W["a"]
            pt, pb = C.bank()
            P.op("pe", lambda e, pt=pt: e.matmul(pt[:, :T], a2[0:64, cs_], t2b_[0:64, :], start=True, stop=True), reads=[a2b, t2bb], writes=[pb])
            P.op("act", lambda e, pt=pt: e.activation(out=a_[:], in_=pt[:, :T], func=AF.Sigmoid, bias=V(V_A0, c)),
                 reads=[pb, vecsb], writes=[ab_])
            pt, pb = C.bank()
            P.op("pe", lambda e, pt=pt: e.matmul(pt[:, :T], g2a[:, cs_], t3a[:], start=True, stop=False), reads=[g2ab, t3ab], writes=[pb])
            P.op("pe", lambda e, pt=pt: e.matmul(pt[:, :T], g2b[0:32, cs_], t3b[0:32, :], start=False, stop=True), reads=[g2bb, t3bb], writes=[pb])
            gg, ggb = W["g"]
            P.op("act", lambda e, pt=pt: e.copy(out=gg[:], in_=pt[:, :T]), reads=[pb], writes=[ggb])
            stores.append(P.dma("sp", g_d[c * 128:(c + 1) * 128, t0:t0 + T], gg[:], reads=[ggb]))
            if has_vres:
                tmp, tmpb = W["tmp"]
                pt, pb = C.bank()
                P.op("pe", lambda e, pt=pt: e.matmul(pt[:, :T], v2[0:32, cs_], t4b_[0:32, :], start=True, stop=True), reads=[v2b, t4bb], writes=[pb])
                P.op("act", lambda e, pt=pt: e.activation(out=tmp[:], in_=pt[:, :T], func=AF.Sigmoid, bias=V(V_V0, c)),
                     reads=[pb, vecsb], writes=[tmpb])
                e1, e1b = W["e1"]
                P.op("pool", lambda e: e.tensor_tensor(out=e1[:], in0=vft[:], in1=v_, op=ALU.subtract), reads=[vftb, vb_], writes=[e1b])
                P.op("pool", lambda e: e.tensor_tensor(out=e1[:], in0=e1[:], in1=tmp[:], op=ALU.mult), reads=[e1b, tmpb], writes=[e1b])
                P.op("pool", lambda e: e.tensor_tensor(out=v_, in0=v_, in1=e1[:], op=ALU.add), reads=[e1b, vb_], writes=[vb_])
            yield
            cs, csb = W["cs"]
            P.op("dve", lambda e: e.tensor_tensor_scan(out=cs[:], data0=segm, data1=sig[:], initial=0.0, op0=ALU.mult, op1=ALU.add),
                 reads=[cstb, sigb], writes=[csb])
            e1, e1b = W["e1"]
            e2, e2b = W["e2"]
            e3, e3b = W["e3"]
            e4, e4b = W["e4"]
            nb, nbb = W["nb"]
            P.op("act", lambda e: e.activation(out=e1[:], in_=cs[:], func=AF.Exp, scale=-KAPPA), reads=[csb], writes=[e1b])
            tmp, tmpb = W["tmp"]
            P.op("pool", lambda e: e.tensor_tensor(out=tmp[:], in0=cs[:], in1=sig[:], op=ALU.subtract), reads=[csb, sigb], writes=[tmpb])
            P.op("act", lambda e: e.activation(out=e2[:], in_=tmp[:], func=AF.Exp, scale=-KAPPA), reads=[tmpb], writes=[e2b])
            P.op("act", lambda e: e.activation(out=e3[:], in_=cs[:], func=AF.Exp, scale=KAPPA), reads=[csb], writes=[e3b])
            for ch in range(CH):
                P.op("pool", lambda e, ch=ch: e.tensor_scalar(out=nb[:, ch:ch + 1], in0=cs[:, ch * 128 + 127:ch * 128 + 128], scalar1=-KAPPA,
                                                              scalar2=None, op0=ALU.mult), reads=[csb], writes=[nbb])
            for ch in range(CH):
                P.op("act", lambda e, ch=ch: e.activation(out=e4[:, ch * 128:(ch + 1) * 128], in_=cs[:, ch * 128:(ch + 1) * 128], func=AF.Exp,
                                                          scale=KAPPA, bias=nb[:, ch:ch + 1]), reads=[csb, nbb], writes=[e4b])
            yield
            kkr, kkrb = W["kkr"]
            kkn, kknb = W["kkn"]
            sqk, sqkb = W["sqk"]
            P.op("act", lambda e: e.activation(out=kkr[:], in_=k_[:], func=AF.Identity, scale=V(V_KK, c)),
                 reads=[kb_, vecsb], writes=[kkrb])
            P.op("act", lambda e: e.activation(out=sqk[:], in_=k_[:], func=AF.Square, scale=V(V_KK, c)),
                 reads=[kb_, vecsb], writes=[sqkb])
            pt, pb = C.bank()
            P.op("pe", lambda e, pt=pt: e.matmul(pt[:, :T], bones[:], sqk[:], start=True, stop=True), reads=[bonesb, sqkb], writes=[pb])
            P.op("act", lambda e, pt=pt: e.activation(out=kkn[:], in_=pt[:, :T], func=AF.Sqrt), reads=[pb], writes=[kknb])
            P.op("dve", lambda e: e.tensor_scalar(out=kkn[:], in0=kkn[:], scalar1=1e-12, scalar2=None, op0=ALU.max), reads=[kknb], writes=[kknb])
            P.op("dve", lambda e: e.reciprocal(out=kkn[:], in_=kkn[:]), reads=[kknb], writes=[kknb])
            P.op("pool", lambda e: e.tensor_tensor(out=kkn[:], in0=kkn[:], in1=kkr[:], op=ALU.mult), reads=[kknb, kkrb], writes=[kknb])
            yield
            kf, kfb = W["kf"]
            P.op("act", lambda e: e.activation(out=kf[:], in_=a_[:], func=AF.Identity, scale=V(V_KA, c), bias=omka[:, c:c + 1]),
                 reads=[ab_, vecsb, omkab], writes=[kfb])
            P.op("pool", lambda e: e.tensor_tensor(out=kf[:], in0=kf[:], in1=k_[:], op=ALU.mult), reads=[kfb, kb_], writes=[kfb])
            rkk, rkkb = W["rkk"]
            P.op("dve", lambda e: e.scalar_tensor_tensor(out=rkk[:], in0=r_[:], scalar=V(V_RK, c), in1=kf[:], op0=ALU.mult, op1=ALU.mult),
                 reads=[rb_, kfb, vecsb], writes=[rkkb])
            pt, pb = C.bank()
            P.op("pe", lambda e, pt=pt: e.matmul(pt[:, :T], bones[:], rkk[:], start=True, stop=True), reads=[bonesb, rkkb], writes=[pb])
            bon, bonb = W["bon"]
            P.op("dve", lambda e, pt=pt: e.tensor_tensor(out=bon[:], in0=pt[:, :T], in1=v_, op=ALU.mult), reads=[pb, vb_], writes=[bonb])
            stores.append(P.dma("sp", bonus_d[c * 128:(c + 1) * 128, t0:t0 + T], bon[:], reads=[bonb]))
            if not has_vres:
                stores.append(P.dma("sp", v_d[c * 128:(c + 1) * 128, t0:t0 + T], v_, reads=[vb_]))
            yield
            arX = W["ar"]
            bsc, bscb = W["bsc"]
            bt, btb = W["bt"]
            kt, ktb = W["kt"]
            bh, bhb = W["bh"]
            kh, khb = W["kh"]

            def v3(ap):
                return ap.rearrange("p (c t) -> p c t", t=128)

            for hh in range(2):
                hs = slice(hh * 64, (hh + 1) * 64)
                arh, arhb = arX[hh]
                P.op("dve", lambda e, hs=hs, arh=arh: e.scalar_tensor_tensor(out=arh[hs, :, 0, :], in0=v3(kkn[hs, :]), scalar=-1.0, in1=v3(e2[hs, :]),
                                                                             op0=ALU.mult, op1=ALU.mult),
                     reads=[kknb, e2b, arhb], writes=[arhb])
                P.op("pool", lambda e, hs=hs, arh=arh: e.tensor_tensor(out=arh[hs, :, 1, :], in0=v3(r_[hs, :]), in1=v3(e1[hs, :]), op=ALU.mult),
                     reads=[rb_, e1b, arhb], writes=[arhb])
            P.op("pool", lambda e: e.tensor_tensor(out=bsc[:], in0=kkn[:], in1=a_[:], op=ALU.mult), reads=[kknb, ab_], writes=[bscb])
            P.op("pool", lambda e: e.tensor_tensor(out=bt[:], in0=bsc[:], in1=e3[:], op=ALU.mult), reads=[bscb, e3b], writes=[btb])
            P.op("dve", lambda e: e.tensor_tensor(out=bh[:], in0=bsc[:], in1=e4[:], op=ALU.mult), reads=[bscb, e4b], writes=[bhb])
            P.op("pool", lambda e: e.tensor_tensor(out=kt[:], in0=kf[:], in1=e3[:], op=ALU.mult), reads=[kfb, e3b], writes=[ktb])
            P.op("dve", lambda e: e.tensor_tensor(out=kh[:], in0=kf[:], in1=e4[:], op=ALU.mult), reads=[kfb, e4b], writes=[khb])
            yield
            bhT, bhTb = W["bhT"]
            khT, khTb = W["khT"]
            vpad, vpadb = W["vpad"]
            for si, (src, srcb) in enumerate(((bh, bhb), (kh, khb), (None, vb_))):
                ptf, ptb = C.bank()
                for ch in range(CH):
                    src_ap = v_[:, ch * 128:(ch + 1) * 128] if src is None else src[:, ch * 128:(ch + 1) * 128]
                    P.op("pe", lambda e, src_ap=src_ap, ch=ch, ptf=ptf: e.transpose(ptf[:, ch * 128:(ch + 1) * 128], src_ap, identf),
                         reads=[srcb, cstb], writes=[ptb])
                if si == 0:
                    P.op("act", lambda e, ptf=ptf: e.copy(out=bhT[:].rearrange("p c t -> p (c t)"), in_=ptf[:, 0:CH * 128]), reads=[ptb], writes=[bhTb])
                elif si == 1:
                    P.op("dve", lambda e, ptf=ptf: e.tensor_copy(out=khT[:].rearrange("p c t -> p (c t)"), in_=ptf[:, 0:CH * 128]), reads=[ptb], writes=[khTb])
                else:
                    for ch in range(CH):
                        for hh in range(2):
                            P.op("act",
                                 lambda e, ptf=ptf, ch=ch, hh=hh: e.copy(
                                     out=vpad[:, ch, hh, 0:64], in_=ptf[:, ch * 128 + hh * 64:ch * 128 + (hh + 1) * 64]),
                                 reads=[ptb, vpadb], writes=[vpadb])
            yield
            Aall, Aallb = W["Aall"]
            bankL, bankLb = C.bank()
            import os
            DBG = os.environ.get("RW_DBG", "")
            for ch in range(CH):
                for hh in range(2):
                    if "nohh1" in DBG and hh == 1:
                        continue
                    u = ch * 2 + hh
                    hs = slice(hh * 64, (hh + 1) * 64)
                    pa, pab = C.bank()
                    ar, arb = arX[hh]
                    arflat = ar[:, ch, :, :].rearrange("p a t -> p (a t)")
                    P.op("pe", lambda e, pa=pa, ch=ch, arflat=arflat: e.matmul(pa[:, 0:256], bt[:, ch * 128:(ch + 1) * 128], arflat, start=True, stop=True),
                         reads=[btb, arb], writes=[pab])
                    P.op("pe", lambda e, pa=pa, ch=ch, arflat=arflat: e.matmul(pa[:, 256:512], kt[:, ch * 128:(ch + 1) * 128], arflat, start=True, stop=True),
                         reads=[ktb, arb], writes=[pab])
                    P.op("dve", lambda e, pa=pa, u=u: e.tensor_tensor(out=Aall[:, u, :], in0=pa[:, :], in1=mask4, op=ALU.mult),
                         reads=[pab, cstb, Aallb], writes=[Aallb])
                    P.op("pe", lambda e, ar=ar, ch=ch, u=u: e.matmul(bankL[:, u * 128:(u + 1) * 128], ar[:, ch, 0, :], bt[:, ch * 128:(ch + 1) * 128], start=True, stop=True),
                         reads=[arb, btb], writes=[bankLb])
            NN = W["NN"]
            LL = W["LL"]
            Tt, Ttb = W["Tt"]
            L0, L0b = LL[0]
            if "noL0" not in DBG:
                P.op("dve", lambda e: e.tensor_tensor(out=L0[:].rearrange("p u t -> p (u t)"), in0=bankL[:, 0:NU * 128], in1=maskL, op=ALU.mult),
                     reads=[bankLb, cstb], writes=[L0b])
            if "noTt" not in DBG:
              P.op("pool", lambda e: e.tensor_tensor(out=Tt[:], in0=Aall[:, :, 0:128], in1=ident4.rearrange("p (u t) -> p u t", t=128), op=ALU.add),
                 reads=[Aallb, cstb], writes=[Ttb])
            return

        def scan_body(c, W):
            e1, e1b = W["e1"]
            arX = W["ar"]
            bhT, bhTb = W["bhT"]
            khT, khTb = W["khT"]
            vpad, vpadb = W["vpad"]
            Aall, Aallb = W["Aall"]
            NN = W["NN"]
            LL = W["LL"]
            Tt, Ttb = W["Tt"]
            L0, L0b = LL[0]
            Nprev = None
            Lprev, Lprevb = L0, L0b
            for lv in range(1, 7):
                Lcur, Lcurb = LL[lv % 2]
                bN, bNb = C.bank()
                bL, bLb = C.bank()
                for u in range(NU):
                    us = slice(u * 128, (u + 1) * 128)
                    nprev_ap = Aall[:, u, 0:128] if Nprev is None else Nprev[0][:, u, :]
                    nprev_b = Aallb if Nprev is None else Nprev[1]
                    if lv < 6:
                        P.op("pe", lambda e, u=u, us=us, nprev_ap=nprev_ap, Lprev=Lprev: e.matmul(bN[:, us], Lprev[:, u, :], nprev_ap, start=True, stop=True),
                             reads=[Lprevb, nprev_b], writes=[bNb])
                    P.op("pe", lambda e, u=u, us=us, nprev_ap=nprev_ap, Lprev=Lprev: e.matmul(bL[:, us], nprev_ap, Lprev[:, u, :], start=True, stop=True),
                         reads=[Lprevb, nprev_b], writes=[bLb])
                if lv < 6:
                    Ncur, Ncurb = NN[lv % 2]
                    P.op("act", lambda e, Ncur=Ncur, bN=bN: e.copy(out=Ncur[:].rearrange("p u t -> p (u t)"), in_=bN[:, 0:NU * 128]), reads=[bNb], writes=[Ncurb])
                    Nprev = (Ncur, Ncurb)
                P.op("act", lambda e, Lcur=Lcur, bL=bL: e.copy(out=Lcur[:].rearrange("p u t -> p (u t)"), in_=bL[:, 0:NU * 128]), reads=[bLb], writes=[Lcurb])
                yield
                bP, bPb = C.bank()
                for u in range(NU):
                    us = slice(u * 128, (u + 1) * 128)
                    P.op("pe", lambda e, u=u, us=us, Lcur=Lcur: e.matmul(bP[:, us], Lcur[:, u, :], Tt[:, u, :], start=True, stop=True),
                         reads=[Lcurb, Ttb], writes=[bPb])
                P.op("dve", lambda e, bP=bP: e.tensor_tensor(out=Tt[:].rearrange("p u t -> p (u t)"), in0=bP[:, 0:NU * 128],
                                                             in1=Tt[:].rearrange("p u t -> p (u t)"), op=ALU.add), reads=[bPb, Ttb], writes=[Ttb])
                Lprev, Lprevb = Lcur, Lcurb
                yield
            yield
            Xs, Xsb = W["Xs"]
            Us, Usb = W["Us"]
            yt, ytb = W["yt"]
            for ch in range(CH):
                bX, bXb = C.bank()
                for hh in range(2):
                    u = ch * 2 + hh
                    hs = slice(hh * 64, (hh + 1) * 64)
                    hc_ = slice(hh * 128, (hh + 1) * 128)
                    P.op("pe", lambda e, u=u, hc_=hc_, ch=ch, hh=hh: e.matmul(bX[:, hc_], Aall[:, u, 256:384], vpad[:, ch, hh, :], start=True, stop=False),
                         reads=[Aallb, vpadb], writes=[bXb])
                    P.op("pe", lambda e, hh=hh, hc_=hc_, ch=ch: e.matmul(bX[:, hc_], arX[hh][0][:, ch, 0, :], S_b[:, c, :], start=False, stop=True),
                         reads=[arX[hh][1], Sbb[c]], writes=[bXb])
                P.op("act", lambda e, bX=bX: e.copy(out=Xs[:], in_=bX[:, 0:256]), reads=[bXb], writes=[Xsb])
                yield
                bU, bUb = C.bank()
                for hh in range(2):
                    u = ch * 2 + hh
                    hc_ = slice(hh * 128, (hh + 1) * 128)
                    P.op("pe", lambda e, u=u, hc_=hc_: e.matmul(bU[:, hc_], Tt[:, u, :], Xs[:, hc_], start=True, stop=True),
                         reads=[Ttb, Xsb], writes=[bUb])
                P.op("dve", lambda e, bU=bU: e.tensor_copy(out=Us[:], in_=bU[:, 0:256]), reads=[bUb], writes=[Usb])
                yield
                bY, bYb = C.bank()
                for hh in range(2):
                    u = ch * 2 + hh
                    hs = slice(hh * 64, (hh + 1) * 64)
                    hc_ = slice(hh * 128, (hh + 1) * 128)
                    P.op("pe", lambda e, hh=hh, hc_=hc_, ch=ch: e.matmul(bY[:, hc_], S_b[:, c, :], arX[hh][0][:, ch, 1, :], start=True, stop=False),
                         reads=[Sbb[c], arX[hh][1]], writes=[bYb])
                    P.op("pe", lambda e, u=u, hc_=hc_: e.matmul(bY[:, hc_], Us[:, hc_], Aall[:, u, 128:256], start=False, stop=False),
                         reads=[Usb, Aallb], writes=[bYb])
                    P.op("pe", lambda e, u=u, hc_=hc_, ch=ch, hh=hh: e.matmul(bY[:, hc_], vpad[:, ch, hh, :], Aall[:, u, 384:512], start=False, stop=True),
                         reads=[vpadb, Aallb], writes=[bYb])
                P.op("act", lambda e, bY=bY, ch=ch: e.copy(out=yt[:, :, ch * 128:(ch + 1) * 128], in_=bY[:, 0:256].rearrange("p (h t) -> p h t", h=2)),
                     reads=[bYb, ytb], writes=[ytb])
                yield
                bS, bSb = C.bank()
                P.op("pe", lambda e, ch=ch: e.matmul(bS[:, 0:256], bhT[:, ch, :], Us[:], start=True, stop=False), reads=[bhTb, Usb], writes=[bSb])
                P.op("pe", lambda e, ch=ch: e.matmul(bS[:, 0:256], khT[:, ch, :], vpad[:, ch, :, :].rearrange("p h v -> p (h v)"), start=False, stop=True),
                     reads=[khTb, vpadb], writes=[bSb])
                for hh in range(2):
                    hs = slice(hh * 64, (hh + 1) * 64)
                    hc_ = slice(hh * 128, (hh + 1) * 128)
                    P.op("dve", lambda e, hs=hs, hc_=hc_, ch=ch, bS=bS: e.scalar_tensor_tensor(
                        out=S_f[hs, c, :], in0=S_f[hs, c, :], scalar=e1[hs, ch * 128 + 127:ch * 128 + 128], in1=bS[hs, hc_], op0=ALU.mult, op1=ALU.add),
                        reads=[Sfb[c], e1b, bSb], writes=[Sfb[c]])
                P.op("act", lambda e: e.copy(out=S_b[:, c, :], in_=S_f[:, c, :]), reads=[Sfb[c]], writes=[Sbb[c]])
            stores.append(P.dma("sp", y0_d[c * 128:(c + 1) * 128, t0:t0 + T].rearrange("(h v) t -> v h t", h=2), yt[0:64, :, :], reads=[ytb]))
            stores.append(P.dma("sp", z_d[c * 128:(c + 1) * 128, t0:t0 + T].rearrange("(h v) t -> v h t", h=2), yt[64:128, :, :], reads=[ytb]))
        fns[it] = (prep_body, scan_body)

    scans = []

    def tile_driver(it):
        yield from prologue(it)
        prep_body, scan_body = fns[it]
        nextp = 0
        running = []
        while nextp < KC or running:
            while len(running) < NPREP and nextp < KC:
                running.append((nextp, prep_body(nextp, psets[nextp % NPREP], ssets[nextp])))
                nextp += 1
            for item in list(running):
                c_, gen = item
                try:
                    next(gen)
                except StopIteration:
                    running.remove(item)
                    scans.append(scan_body(c_, ssets[c_]))
            yield

    def step_scans():
        for gn_ in list(scans):
            try:
                next(gn_)
            except StopIteration:
                scans.remove(gn_)

    for it in range(NT):
        drv = tile_driver(it)
        while True:
            try:
                next(drv)
            except StopIteration:
                break
            step_scans()
    while scans:
        step_scans()
    stores.append(P.dma("sp", state_d, S_f[:], reads=Sfb, semkey=Sfb[0]))
    return stores


GN_EPS = 64e-5


def phase_rwkv_post(P, C, hT_in, hT_out, y0_d, z_d, bonus_d, g_d, states_d, mvec_d, w_o, vecs_d, cst_d, NT, T):
    A = P.alloc
    wo, wob = A("po_wo", (128, KC, D), BF16)
    P.dma("pool", wo[:], w_o.rearrange("(k p) n -> p k n", p=128), writes=[wob])
    vecs, vecsb = A("po_vecs", (128, NVEC * 8), F32)
    P.dma("sp", vecs[:], vecs_d, writes=[vecsb])
    cst, cstb = A("po_cst", (128, CST_W), F32)
    P.dma("sp", cst[:], cst_d, writes=[cstb])
    identf = cst[:, 1024:1152]
    bonesf = cst[:, 1152:1280]
    G, Gb = A("po_G", (128, 3, KC, 128), F32)
    P.dma("sp", G[:], states_d.rearrange("j p c x -> p j c x"), writes=[Gb])
    mv, mvb = A("po_mv", (128, 6), F32)
    P.dma("sp", mv[:], mvec_d, writes=[mvb])
    gne, gneb = A("po_gne", (128, 1), F32)
    P.op("pool", lambda e: e.memset(gne[:], GN_EPS), writes=[gneb])
    SS = [A("po_SS%d" % c, (128, 128), F32) for c in range(KC)]
    for c in range(KC):
        P.op("pool", lambda e, c=c: e.memset(SS[c][0][:], 0.0), writes=[SS[c][1]])
    BD = [A("po_BD%d" % i, (128, 128), F32) for i in range(2)]
    QB = [A("po_QB%d" % i, (128, 128), F32) for i in range(2)]
    PB = [A("po_PB%d" % i, (128, 128), F32) for i in range(2)]
    for i in range(2):
        P.op("pool", lambda e, i=i: e.memset(BD[i][0][:], 0.0), writes=[BD[i][1]])
        P.op("pool", lambda e, i=i: e.memset(QB[i][0][:], 0.0), writes=[QB[i][1]])
    n = 0
    for j in range(3):
        for c in range(KC):
            bd, bdb = BD[n % 2]
            qb, qbb = QB[n % 2]
            pbt, pbb = PB[n % 2]
            n += 1
            for hh in range(2):
                hs = slice(hh * 64, (hh + 1) * 64)
                P.op("pool", lambda e, hs=hs, j=j, c=c, bd=bd: e.tensor_scalar(out=bd[hs, hs], in0=G[hs, j, c, 64:128], scalar1=mv[hs, j:j + 1], scalar2=None, op0=ALU.mult),
                     reads=[Gb, mvb, bdb], writes=[bdb])
                P.op("pool", lambda e, hs=hs, j=j, c=c, qb=qb: e.tensor_scalar(out=qb[hs, hs], in0=G[hs, j, c, 0:64], scalar1=mv[hs, j:j + 1], scalar2=None, op0=ALU.mult),
                     reads=[Gb, mvb, qbb], writes=[qbb])
            P.op("dve", lambda e, j=j, bd=bd: e.scalar_tensor_tensor(out=bd[:], in0=identf, scalar=mv[:, 3 + j:4 + j], in1=bd[:], op0=ALU.mult, op1=ALU.add),
                 reads=[cstb, mvb, bdb], writes=[bdb])
            pt, pb = C.bank()
            P.op("pe", lambda e, pt=pt, bd=bd: e.transpose(pt[:, 0:128], bd[:], identf), reads=[bdb, cstb], writes=[pb])
            P.op("act", lambda e, pt=pt, pbt=pbt: e.copy(out=pbt[:], in_=pt[:, 0:128]), reads=[pb], writes=[pbb])
            pt2, pb2 = C.bank()
            P.op("pe", lambda e, pt2=pt2, pbt=pbt, c=c: e.matmul(pt2[:, 0:128], pbt[:], SS[c][0][:], start=True, stop=True), reads=[pbb, SS[c][1]], writes=[pb2])
            P.op("dve", lambda e, pt2=pt2, qb=qb, c=c: e.tensor_tensor(out=SS[c][0][:], in0=pt2[:, 0:128], in1=qb[:], op=ALU.add), reads=[pb2, qbb], writes=[SS[c][1]])

    def V(i, kc):
        return vecs[:, i * 8 + kc:i * 8 + kc + 1]

    tiles = [[A("po_%s%d" % (nm, i), (128, KC, T), F32) for nm in ("h", "bo", "g", "y0", "z")] for i in range(2)]
    yg, _ = A("po_yg", (128, KC, T), BF16)
    ygb = [P.buf("poyg%d" % k) for k in range(KC)]
    sc = [[A("po_s%d_%d" % (i, k), (128, T), F32) for k in range(3)] for i in range(KC)]
    hin = hT_in.rearrange("(k p) t -> p k t", p=128)
    hout = hT_out.rearrange("(k p) t -> p k t", p=128)

    def fm(d):
        return d.rearrange("(k p) t -> p k t", p=128)

    stores = []
    for it in range(NT):
        t0 = it * T
        (h, hb), (bo, bob), (gg, ggb), (y0, y0b), (zz, zzb) = tiles[it % 2]
        P.dma("sp", h[:], hin[:, :, HALO + t0:HALO + t0 + T], writes=[hb])
        P.dma("sp", bo[:], fm(bonus_d)[:, :, t0:t0 + T], writes=[bob])
        P.dma("sp", gg[:], fm(g_d)[:, :, t0:t0 + T], writes=[ggb])
        P.dma("sp", y0[:], fm(y0_d)[:, :, t0:t0 + T], writes=[y0b])
        P.dma("sp", zz[:], fm(z_d)[:, :, t0:t0 + T], writes=[zzb])
        def chain(c):
            (y, yb), (d, db), (q, qb_) = sc[c]
            yield
            pt, pb = C.bank()
            P.op("pe", lambda e, pt=pt, c=c: e.matmul(pt[:, :T], SS[c][0][:], zz[:, c, :], start=True, stop=True), reads=[SS[c][1], zzb], writes=[pb])
            yield
            P.op("dve", lambda e, pt=pt, c=c, y=y: e.tensor_tensor(out=y[:], in0=pt[:, :T], in1=y0[:, c, :], op=ALU.add), reads=[pb, y0b], writes=[yb])
            yield
            pt, pb = C.bank()
            P.op("pe", lambda e, pt=pt, y=y: e.matmul(pt[:, :T], bonesf, y[:], start=True, stop=True), reads=[cstb, yb], writes=[pb])
            yield
            P.op("dve", lambda e, pt=pt, y=y, d=d: e.scalar_tensor_tensor(out=d[:], in0=pt[:, :T], scalar=-1.0 / 64, in1=y[:], op0=ALU.mult, op1=ALU.add),
                 reads=[pb, yb], writes=[db])
            P.op("act", lambda e, d=d, q=q: e.activation(out=q[:], in_=d[:], func=AF.Square), reads=[db], writes=[qb_])
            yield
            pt, pb = C.bank()
            P.op("pe", lambda e, pt=pt, q=q: e.matmul(pt[:, :T], bonesf, q[:], start=True, stop=True), reads=[cstb, qb_], writes=[pb])
            yield
            P.op("act", lambda e, pt=pt, q=q: e.activation(out=q[:], in_=pt[:, :T], func=AF.Sqrt, bias=gne[:, 0:1], scale=1.0 / 64), reads=[pb, gneb], writes=[qb_])
            yield
            P.op("dve", lambda e, q=q: e.reciprocal(out=q[:], in_=q[:]), reads=[qb_], writes=[qb_])
            yield
            P.op("dve", lambda e, d=d, q=q: e.tensor_tensor(out=d[:], in0=d[:], in1=q[:], op=ALU.mult), reads=[db, qb_], writes=[db])
            yield
            P.op("act", lambda e, d=d, c=c: e.activation(out=d[:], in_=d[:], func=AF.Identity, scale=V(V_LNW, c), bias=V(V_LNB, c)),
                 reads=[db, vecsb], writes=[db])
            yield
            P.op("pool", lambda e, d=d, c=c: e.tensor_tensor(out=d[:], in0=d[:], in1=bo[:, c, :], op=ALU.add), reads=[db, bob], writes=[db])
            yield
            P.op("dve", lambda e, d=d, c=c: e.tensor_tensor(out=yg[:, c, :], in0=d[:], in1=gg[:, c, :], op=ALU.mult), reads=[db, ggb], writes=[ygb[c]])
        gens = [chain(c) for c in range(KC)]
        while gens:
            for gn_ in list(gens):
                try:
                    next(gn_)
                except StopIteration:
                    gens.remove(gn_)
        for dc in range(KC):
            po, pob = C.bank()
            for c in range(KC):
                P.op("pe", lambda e, c=c, dc=dc, po=po: e.matmul(po[:, :T], wo[:, c, dc * 128:(dc + 1) * 128], yg[:, c, :], start=(c == 0), stop=(c == KC - 1)),
                     reads=[wob, ygb[c]], writes=[pob])
            P.op("dve", lambda e, dc=dc, po=po: e.tensor_tensor(out=h[:, dc, :], in0=po[:, :T], in1=h[:, dc, :], op=ALU.add), reads=[pob, hb], writes=[hb])
        stores.append(P.dma("sp", hout[:, :, t0:t0 + T], h[:], reads=[hb]))
    return stores


def _mk(nc):
    def din(name, shape):
        return nc.dram_tensor(name, list(shape), F32, kind="ExternalInput").ap()

    def dout(name, shape):
        return nc.dram_tensor(name, list(shape), F32, kind="ExternalOutput").ap()

    def dint(name, shape):
        return nc.dram_tensor(name, list(shape), F32).ap()
    return din, dout, dint


def _ffn_inputs(din):
    return din("wg", [D, FH]), din("wu", [D, FH]), din("wd", [FH, D]), din("gn", [D])


def build_even(NTOK, final):
    nc = bass.Bass("TRN2", target_bir_lowering=False)
    din, dout, dint = _mk(nc)
    T = 512
    hT = din("hT", [D, HALO + NTOK])
    w_in = din("w_in", [D, 1536]); w_out = din("w_out", [D, D]); mg = din("mg", [D])
    sg = din("sg", [1, 512]); wsT = din("wsT", [128, 4, 128]); sbias = din("sbias", [1, 512]); pw = din("pw", [128, 4, 128])
    psc = din("psc", [512]); invc = din("invc", [128, 4, HALO])
    wg, wu, wd, gn = _ffn_inputs(din)
    fn = din("fn", [D]) if final else None
    hA = dint("hA", [D, NTOK])
    hO = dout("hO", [D, NTOK])
    with ExitStack() as stack:
        P = Prog(nc, stack)
        C = Ctx(P, T)
        P.persist()
        block = stack.enter_context(nc.Block())
        phase_ab(P, C, hT, hA, w_in, w_out, mg, sg, wsT, sbias, pw, psc, invc, NTOK // T)
        P.phase_reset()
        if final:
            stores = phase_ffn(P, C, hA, None, wg, wu, wd, gn, NTOK // T, final_gain=fn, outT=hO)
        else:
            stores = phase_ffn(P, C, hA, hO, wg, wu, wd, gn, NTOK // T)
        P.emit(block, final_waits=stores)
    return nc


_RW_NAMES = ("w_r", "w_k", "w_v", "w1", "w2", "a1", "a2", "g1", "g2", "v1", "v2")
_RW_SHAPES = {"w_r": [D, D], "w_k": [D, D], "w_v": [D, D], "w1": [D, 64], "w2": [64, D], "a1": [D, 64], "a2": [64, D],
              "g1": [D, 160], "g2": [160, D], "v1": [D, 32], "v2": [32, D]}


def build_odd_pre(NTOK, has_vres):
    nc = bass.Bass("TRN2", target_bir_lowering=False)
    din, dout, dint = _mk(nc)
    T = 256
    hT = din("hT", [D, HALO + NTOK])
    Wd = {n: din(n, _RW_SHAPES[n]) for n in _RW_NAMES if has_vres or n not in ("v1", "v2")}
    vecs = din("vecs", [128, NVEC * 8]); cst = din("cst", [128, CST_W])
    vfirst = din("vfirst", [D, NTOK]) if has_vres else None
    y0 = dout("y0", [D, NTOK]); z = dout("z", [D, NTOK]); bonus = dout("bonus", [D, NTOK]); g = dout("g", [D, NTOK])
    v = dout("v", [D, NTOK]) if not has_vres else None
    state = dout("state", [128, 8, 128])
    with ExitStack() as stack:
        P = Prog(nc, stack)
        C = Ctx(P, 512)
        P.persist()
        block = stack.enter_context(nc.Block())
        stores = phase_rwkv_pre(P, C, hT, Wd, vecs, cst, NTOK // T, T, y0, z, bonus, g, v, vfirst, state, has_vres)
        P.emit(block, final_waits=stores)
    return nc


def build_odd_post(NTOK, final):
    nc = bass.Bass("TRN2", target_bir_lowering=False)
    din, dout, dint = _mk(nc)
    T = 512
    hT = din("hT", [D, HALO + NTOK])
    y0 = din("y0", [D, NTOK]); z = din("z", [D, NTOK]); bonus = din("bonus", [D, NTOK]); g = din("g", [D, NTOK])
    states = din("states", [3, 128, 8, 128]); mvec = din("mvec", [128, 6])
    w_o = din("w_o", [D, D]); vecs = din("vecs", [128, NVEC * 8]); cst = din("cst", [128, CST_W])
    wg, wu, wd, gn = _ffn_inputs(din)
    fn = din("fn", [D]) if final else None
    hP = dint("hP", [D, NTOK])
    hO = dout("hO", [D, NTOK])
    with ExitStack() as stack:
        P = Prog(nc, stack)
        C = Ctx(P, T)
        P.persist()
        block = stack.enter_context(nc.Block())
        phase_rwkv_post(P, C, hT, hP, y0, z, bonus, g, states, mvec, w_o, vecs, cst, NTOK // T, T)
        P.phase_reset()
        if final:
            stores = phase_ffn(P, C, hP, None, wg, wu, wd, gn, NTOK // T, final_gain=fn, outT=hO)
        else:
            stores = phase_ffn(P, C, hP, hO, wg, wu, wd, gn, NTOK // T)
        P.emit(block, final_waits=stores)
    return nc


def _pack_vecs(inp, i, layer):
    vs = [inp["mix_norm"][layer]] + [inp["rwkv_mu"][i][j] for j in range(6)] + [
        inp["rwkv_w0"][i], inp["rwkv_a0"][i], inp["rwkv_k_k"][i], inp["rwkv_k_a"][i], np.asarray(inp["rwkv_r_k"][i]).reshape(-1),
        inp["rwkv_v0"][i - 1] if i > 0 else np.zeros(D, np.float32), inp["rwkv_ln_w"][i], inp["rwkv_ln_b"][i]]
    out = np.zeros((128, NVEC * 8), np.float32)
    for j, v in enumerate(vs):
        out[:, j * 8:(j + 1) * 8] = np.asarray(v, np.float32).reshape(8, 128).T
    return out


def _invc_table(first):
    t = np.zeros((128, 4, HALO), np.float32)
    for g, w in enumerate((2, 4, 8, 16)):
        for i in range(HALO):
            t[:, g, i] = 1.0 / (min(i + 1, w) if first else w)
    return t


def _with_halo(hT_list, G):
    out = []
    for c, h in enumerate(hT_list):
        buf = np.zeros((D, HALO + h.shape[1]), np.float32)
        buf[:, HALO:] = h
        if c % G != 0:
            buf[:, :HALO] = hT_list[c - 1][:, -HALO:]
        out.append(buf)
    return out


def run_network(inp, B, G, NTOK, depth=4):
    NC = B * G
    cores = list(range(NC))
    inp = {k: np.asarray(v, np.float32) for k, v in inp.items()}
    x = inp["x"]
    hT = [np.ascontiguousarray(x[c // G, (c % G) * NTOK:(c % G + 1) * NTOK].T) for c in cores]
    cst = make_cst()
    vfirst = None
    cache = {}

    def launch(key, builder, maps):
        if key not in cache:
            cache[key] = builder()
        return run_bass_kernel_spmd(cache[key], maps, core_ids=cores).results

    def ffn_w(layer):
        return {"wg": inp["ffn_w_gate"][layer], "wu": inp["ffn_w_up"][layer], "wd": inp["ffn_w_down"][layer], "gn": inp["ffn_norm"][layer]}

    for layer in range(depth):
        i = layer // 2
        final = layer == depth - 1
        hh = _with_halo(hT, G)
        if layer % 2 == 0:
            common = {"w_in": inp["ab_w_in"][i], "w_out": inp["ab_w_out"][i], "mg": inp["mix_norm"][layer], "sg": inp["sgu_gain"][i][None],
                      "wsT": np.ascontiguousarray(inp["sgu_w_s"][i].transpose(2, 0, 1)), "sbias": inp["sgu_bias"][i].reshape(1, 512),
                      "pw": np.ascontiguousarray(inp["pool_w"][i].transpose(1, 0, 2)), "psc": inp["pool_scale"][i]}
            common.update(ffn_w(layer))
            if final:
                common["fn"] = inp["final_norm"]
            maps = [dict(common, hT=hh[c], invc=_invc_table(c % G == 0)) for c in cores]
            res = launch(("even", final), lambda: build_even(NTOK, final), maps)
            hT = [r["hO"] for r in res]
        else:
            has_vres = i > 0
            vecs = _pack_vecs(inp, i, layer)
            common = {n: inp["rwkv_" + n][i] for n in _RW_NAMES if n not in ("v1", "v2")}
            if has_vres:
                common["v1"] = inp["rwkv_v1"][i - 1]
                common["v2"] = inp["rwkv_v2"][i - 1]
            common.update(vecs=vecs, cst=cst)
            maps = [dict(common, hT=hh[c], **({"vfirst": vfirst[c]} if has_vres else {})) for c in cores]
            res = launch(("pre", has_vres), lambda: build_odd_pre(NTOK, has_vres), maps)
            if not has_vres:
                vfirst = [r["v"] for r in res]
            common2 = {"w_o": inp["rwkv_w_o"][i], "vecs": vecs, "cst": cst}
            common2.update(ffn_w(layer))
            if final:
                common2["fn"] = inp["final_norm"]
            maps2 = []
            for c in cores:
                q = c % G
                st = np.zeros((3, 128, 8, 128), np.float32)
                mv = np.zeros((128, 6), np.float32)
                mv[:, 3:6] = 1.0
                for j in range(min(q, 3)):
                    st[j] = res[c - q + j]["state"]
                    mv[:, j] = 1.0
                    mv[:, 3 + j] = 0.0
                maps2.append(dict(common2, hT=hh[c], y0=res[c]["y0"], z=res[c]["z"], bonus=res[c]["bonus"], g=res[c]["g"], states=st, mvec=mv))
            res2 = launch(("post", final), lambda: build_odd_post(NTOK, final), maps2)
            hT = [r["hO"] for r in res2]
    out = np.zeros((B, G * NTOK, D), np.float32)
    for c in cores:
        out[c // G, (c % G) * NTOK:(c % G + 1) * NTOK] = hT[c].T
    return out


def exchange_halo(P, xin_d, xg_d, sel_d, h_dst, groups):
    G = len(groups[0])
    xgb = P.buf("dram_xg")
    P.coll("AllGather", xin_d, xg_d, groups, writes=[xgb])
    xs, xsb = P.alloc("ex_xs", (128, G, KC, HALO), F32)
    for r in range(G):
        P.dma("sp", xs[:, r, :, :], xg_d[r * D:(r + 1) * D, :].rearrange("(k p) c -> p k c", p=128), reads=[xgb], writes=[xsb])
    sel, selb = P.alloc("ex_sel", (128, 4), F32)
    P.dma("sp", sel[:], sel_d, writes=[selb])
    hal, halb = P.alloc("ex_hal", (128, KC, HALO), F32)
    P.op("dve", lambda e: e.tensor_scalar(out=hal[:], in0=xs[:, 0, :, :], scalar1=sel[:, 0:1], scalar2=None, op0=ALU.mult),
         reads=[xsb, selb], writes=[halb])
    for r in range(1, G):
        P.op("dve", lambda e, r=r: e.scalar_tensor_tensor(out=hal[:], in0=xs[:, r, :, :], scalar=sel[:, r:r + 1], in1=hal[:], op0=ALU.mult, op1=ALU.add),
             reads=[xsb, selb, halb], writes=[halb])
    P.dma("sp", h_dst[:, 0:HALO].rearrange("(k p) c -> p k c", p=128), hal[:], reads=[halb])


def build_fused(NTOK, B, G, depth=4):
    nc = bass.Bass("TRN2", target_bir_lowering=False)
    din, dout, dint = _mk(nc)
    groups = [list(range(b * G, (b + 1) * G)) for b in range(B)]
    hbuf = [din("hT", [D, HALO + NTOK]), dint("hB", [D, HALO + NTOK])]
    hM = dint("hM", [D, NTOK])
    invc = din("invc", [128, 4, HALO]); sel = din("sel", [128, 4]); mvec = din("mvec", [128, 6]); cst = din("cst", [128, CST_W])
    xin = dint("xin", [D, HALO]); xg = dint("xg", [G * D, HALO])
    y0 = dint("y0", [D, NTOK]); z = dint("z", [D, NTOK]); bonus = dint("bonus", [D, NTOK]); g = dint("g", [D, NTOK]); vf = dint("vf", [D, NTOK])
    state = dint("state", [128, 1024]); sg_ = dint("sgath", [max(G, 3) * 128, 1024])
    out = dout("outT", [D, NTOK])
    W = {}
    for layer in range(depth):
        L = "L%d_" % layer
        for n, sh in (("wg", [D, FH]), ("wu", [D, FH]), ("wd", [FH, D]), ("gn", [D])):
            W[L + n] = din(L + n, sh)
        if layer % 2 == 0:
            for n, sh in (("w_in", [D, 1536]), ("w_out", [D, D]), ("mg", [D]), ("sg", [1, 512]), ("wsT", [128, 4, 128]), ("sbias", [1, 512]),
                          ("pw", [128, 4, 128]), ("psc", [512])):
                W[L + n] = din(L + n, sh)
        else:
            for n in _RW_NAMES:
                if n in ("v1", "v2") and layer < 2:
                    continue
                W[L + n] = din(L + n, _RW_SHAPES[n])
            W[L + "w_o"] = din(L + "w_o", [D, D])
            W[L + "vecs"] = din(L + "vecs", [128, NVEC * 8])
    fn = din("fn", [D])
    with ExitStack() as stack:
        P = Prog(nc, stack)
        C = Ctx(P, 512)
        P.persist()
        block = stack.enter_context(nc.Block())
        cur = 0
        stores = []
        if G < 3:
            zt, ztb = P.alloc("zfill", (128, 1024), F32)
            P.op("pool", lambda e: e.memset(zt[:], 0.0), writes=[ztb])
            for r in range(G, 3):
                P.dma("sp", sg_[r * 128:(r + 1) * 128, :], zt[:], reads=[ztb])
            P.phase_reset()
        for layer in range(depth):
            L = "L%d_" % layer
            final = layer == depth - 1
            hin = hbuf[cur]
            hnext = hbuf[1 - cur]
            if layer > 0:
                exchange_halo(P, xin, xg, sel, hin, groups)
                P.phase_reset()
            if layer % 2 == 0:
                phase_ab(P, C, hin, hM, W[L + "w_in"], W[L + "w_out"], W[L + "mg"], W[L + "sg"], W[L + "wsT"], W[L + "sbias"], W[L + "pw"],
                         W[L + "psc"], invc, NTOK // 512)
                P.phase_reset()
            else:
                has_vres = layer >= 3
                Wd = {n: W[L + n] for n in _RW_NAMES if (L + n) in W}
                phase_rwkv_pre(P, C, hin, Wd, W[L + "vecs"], cst, NTOK // RW_T, RW_T, y0, z, bonus, g, vf, vf, state.rearrange("p (c x) -> p c x", c=8), has_vres)
                P.phase_reset()
                sgb = P.buf("dram_sg")
                P.coll("AllGather", state, sg_[0:G * 128, :], groups, writes=[sgb])
                P.phase_reset()
                phase_rwkv_post(P, C, hin, hM, y0, z, bonus, g, sg_[0:3 * 128, :].rearrange("(j p) (c x) -> j p c x", p=128, c=8), mvec,
                                W[L + "w_o"], W[L + "vecs"], cst, NTOK // 256, 256)
                P.phase_reset()
            if final:
                stores = phase_ffn(P, C, hM, None, W[L + "wg"], W[L + "wu"], W[L + "wd"], W[L + "gn"], NTOK // 512, final_gain=fn, outT=out)
            else:
                phase_ffn(P, C, hM, hnext[:, HALO:HALO + NTOK], W[L + "wg"], W[L + "wu"], W[L + "wd"], W[L + "gn"], NTOK // 512, halo_out=xin)
                P.phase_reset()
            cur = 1 - cur
        P.emit(block, final_waits=stores)
    return nc


def run_fused(inp, B, G, NTOK, depth=4):
    NC = B * G
    cores = list(range(NC))
    inp = {k: np.asarray(v, np.float32) for k, v in inp.items()}
    x = inp["x"]
    hT = [np.ascontiguousarray(x[c // G, (c % G) * NTOK:(c % G + 1) * NTOK].T) for c in cores]
    hh = _with_halo(hT, G)
    common = {"cst": make_cst(), "fn": inp["final_norm"]}
    for layer in range(depth):
        L = "L%d_" % layer
        i = layer // 2
        common[L + "wg"] = inp["ffn_w_gate"][layer]
        common[L + "wu"] = inp["ffn_w_up"][layer]
        common[L + "wd"] = inp["ffn_w_down"][layer]
        common[L + "gn"] = inp["ffn_norm"][layer]
        if layer % 2 == 0:
            common[L + "w_in"] = inp["ab_w_in"][i]
            common[L + "w_out"] = inp["ab_w_out"][i]
            common[L + "mg"] = inp["mix_norm"][layer]
            common[L + "sg"] = inp["sgu_gain"][i][None]
            common[L + "wsT"] = np.ascontiguousarray(inp["sgu_w_s"][i].transpose(2, 0, 1))
            common[L + "sbias"] = inp["sgu_bias"][i].reshape(1, 512)
            common[L + "pw"] = np.ascontiguousarray(inp["pool_w"][i].transpose(1, 0, 2))
            common[L + "psc"] = inp["pool_scale"][i]
        else:
            for n in _RW_NAMES:
                if n in ("v1", "v2"):
                    if i > 0:
                        common[L + n] = inp["rwkv_" + n][i - 1]
                else:
                    common[L + n] = inp["rwkv_" + n][i]
            common[L + "w_o"] = inp["rwkv_w_o"][i]
            common[L + "vecs"] = _pack_vecs(inp, i, layer)
    maps = []
    for c in cores:
        q = c % G
        sel = np.zeros((128, 4), np.float32)
        if q > 0:
            sel[:, q - 1] = 1.0
        mv = np.zeros((128, 6), np.float32)
        mv[:, 3:6] = 1.0
        for j in range(min(q, 3)):
            mv[:, j] = 1.0
            mv[:, 3 + j] = 0.0
        maps.append(dict(common, hT=hh[c], invc=_invc_table(q == 0), sel=sel, mvec=mv))
    nc = build_fused(NTOK, B, G, depth)
    res = run_bass_kernel_spmd(nc, maps, core_ids=cores).results
    out = np.zeros((B, G * NTOK, D), np.float32)
    for c in cores:
        out[c // G, (c % G) * NTOK:(c % G + 1) * NTOK] = res[c]["outT"].T
    return out


def kernel(**inputs):
    return run_fused(inputs, 2, 4, 4096)
```
